# Optimizing a Trainium2 kernel written in Bass

```python
import jax, jax.numpy as jnp
from jax import lax
import numpy as np

D_MODEL = 1024
BATCH = 8
SEQ = 2048
DEPTH = 4
DEC_BATCH = 128
DEC_SEQ = 4
PAST_LEN = 16384
PAGE_SIZE = 128

N_MIXERS = 2
CONV_WIDTH = 3
E_CONV = D_MODEL
HG_EXPAND = 128
HG_HEADS = D_MODEL // HG_EXPAND
HG_DK = HG_EXPAND
HG_DV = D_MODEL // HG_HEADS
HG_F = HG_HEADS * HG_DK
HG_I = HG_HEADS * HG_DV
CHUNK = 32
EPS = 1e-6
N_CONV_LAYERS = (DEPTH + 1) // 2
N_HGRN_LAYERS = DEPTH // 2

kernel_name = "hybrid_shortconv_hgrn2_adaln_step"


def rmsnorm(x, g):
    xf = x.astype(jnp.float32)
    y = xf * lax.rsqrt(jnp.mean(xf * xf, axis=-1, keepdims=True) + EPS)
    return (y * g.astype(jnp.float32)).astype(x.dtype)


def ada_mod(c, w, b):
    m = jnp.einsum('bd,de->be', jax.nn.silu(c), w) + b
    shift, scale, gate = jnp.split(m, 3, axis=-1)
    return shift[:, None], scale[:, None], gate[:, None]


def short_conv_mixer(h, conv_state, w_in, conv_w, w_out):
    T = h.shape[1]
    proj = jnp.einsum('btd,de->bte', h, w_in)
    b_gate, c_gate, v, z = jnp.split(proj, 4, axis=-1)
    u = c_gate * v
    u_ext = jnp.concatenate([conv_state.astype(u.dtype), u], axis=1)
    conv = sum(u_ext[:, k:k + T] * conv_w[k] for k in range(CONV_WIDTH))
    y = b_gate * conv * jax.nn.silu(z)
    out = jnp.einsum('bte,ed->btd', y, w_out)
    return out, u_ext[:, -(CONV_WIDTH - 1):]


def _to_chunks(a, n, L):
    B = a.shape[0]
    return jnp.moveaxis(a.reshape((B, n, L) + a.shape[2:]), 1, 0)


def hgrn2_recurrence(q, k, v, logf, s0):
    B, T = q.shape[0], q.shape[1]
    L = min(CHUNK, T)
    n = -(-T // L)
    pad = n * L - T
    if pad:
        pw = ((0, 0), (0, pad), (0, 0), (0, 0))
        q, k, v, logf = (jnp.pad(a, pw) for a in (q, k, v, logf))
    qs, ks, vs, gs = (_to_chunks(a, n, L) for a in (q, k, v, logf))
    causal = jnp.tril(jnp.ones((L, L), dtype=bool))[None, :, :, None, None]

    def step(S, inp):
        qc, kc, vc, gc = inp
        G = jnp.cumsum(gc, axis=1)
        o_inter = jnp.einsum('blhk,bhkv->blhv', qc * jnp.exp(G), S)
        diff = jnp.where(causal, G[:, :, None] - G[:, None, :], -jnp.inf)
        A = jnp.einsum('bthk,btshk,bshk->bhts', qc, jnp.exp(diff), kc)
        o_intra = jnp.einsum('bhts,bshv->bthv', A, vc)
        G_last = G[:, -1]
        S_new = S * jnp.exp(G_last)[..., None] + jnp.einsum(
            'bshk,bshv->bhkv', kc * jnp.exp(G_last[:, None] - G), vc)
        return S_new, o_inter + o_intra

    S_fin, o = lax.scan(step, s0, (qs, ks, vs, gs))
    o = jnp.moveaxis(o, 0, 1).reshape(B, n * L, q.shape[2], v.shape[-1])[:, :T]
    return o, S_fin


def hgrn2_mixer(h, state, w_in, lb, onorm_g, w_out):
    B, T = h.shape[0], h.shape[1]
    proj = jnp.einsum('btd,de->bte', h, w_in)
    q, fpre, i, z = jnp.split(proj, [HG_F, 2 * HG_F, 2 * HG_F + HG_I], axis=-1)
    fpre = fpre.astype(jnp.float32)
    lbf = lb.astype(jnp.float32)
    logf = jnp.logaddexp(jnp.log(lbf), jnp.log1p(-lbf) + jax.nn.log_sigmoid(fpre))
    k = (1.0 - lbf) * jax.nn.sigmoid(-fpre)
    qf = jax.nn.silu(q.astype(jnp.float32))
    hs = lambda a, d: a.reshape(B, T, HG_HEADS, d)
    o, S_fin = hgrn2_recurrence(hs(qf, HG_DK), hs(k, HG_DK), hs(i.astype(jnp.float32), HG_DV),
                                hs(logf, HG_DK), state.astype(jnp.float32))
    o = o * lax.rsqrt(jnp.mean(o * o, axis=-1, keepdims=True) + EPS)
    o = (o * onorm_g.astype(jnp.float32).reshape(HG_HEADS, HG_DV)).reshape(B, T, HG_I)
    y = o.astype(h.dtype) * jax.nn.silu(z)
    return jnp.einsum('bte,ed->btd', y, w_out), S_fin


def trunk(x, c, conv_states, hgrn_states, norm_g, w_ada, b_ada, conv_w_in, conv_w, conv_w_out,
          hgrn_w_in, hgrn_lower_bounds, hgrn_onorm_g, hgrn_w_out, final_norm_g):
    p = jax.nn.softmax(hgrn_lower_bounds.astype(jnp.float32), axis=0)
    lbs = jnp.concatenate([jnp.zeros_like(p[:1]), jnp.cumsum(p[1:], axis=0)], axis=0)
    new_conv, new_hgrn = [], []
    for layer in range(DEPTH):
        shift, scale, gate = ada_mod(c, w_ada[layer], b_ada[layer])
        h = rmsnorm(x, norm_g[layer]) * (1.0 + scale) + shift
        j = layer // N_MIXERS
        if layer % N_MIXERS == 0:
            out, st = short_conv_mixer(h, conv_states[j], conv_w_in[j], conv_w[j], conv_w_out[j])
            new_conv.append(st)
        else:
            out, st = hgrn2_mixer(h, hgrn_states[j], hgrn_w_in[j], lbs[layer], hgrn_onorm_g[j], hgrn_w_out[j])
            new_hgrn.append(st)
        x = x + gate * out.astype(x.dtype)
    return rmsnorm(x, final_norm_g), jnp.stack(new_conv), jnp.stack(new_hgrn)


def setup_inputs(seed: int = 0) -> dict:
    key = jax.random.key(seed)
    ks = jax.random.split(key, 20)
    f32 = jnp.float32
    nrm = lambda k, shape, s: jax.random.normal(k, shape, f32) * s
    return {
        "x_prompt": nrm(ks[0], (BATCH, SEQ, D_MODEL), 1.0),
        "x_sample": nrm(ks[1], (DEC_BATCH, DEC_SEQ, D_MODEL), 1.0),
        "state_conv": nrm(ks[2], (N_CONV_LAYERS, DEC_BATCH, CONV_WIDTH - 1, E_CONV), 1.0),
        "state_hgrn": nrm(ks[3], (N_HGRN_LAYERS, DEC_BATCH, HG_HEADS, HG_DK, HG_DV), 0.5),
        "c_prompt": nrm(ks[4], (BATCH, D_MODEL), 1.0),
        "c_sample": nrm(ks[5], (DEC_BATCH, D_MODEL), 1.0),
        "norm_g": 1.0 + nrm(ks[6], (DEPTH, D_MODEL), 0.02),
        "w_ada": nrm(ks[7], (DEPTH, D_MODEL, 3 * D_MODEL), 0.3 * D_MODEL ** -0.5),
        "b_ada": nrm(ks[8], (DEPTH, 3 * D_MODEL), 0.02),
        "conv_w_in": nrm(ks[9], (N_CONV_LAYERS, D_MODEL, 4 * E_CONV), D_MODEL ** -0.5),
        "conv_w": nrm(ks[10], (N_CONV_LAYERS, CONV_WIDTH, E_CONV), CONV_WIDTH ** -0.5),
        "conv_w_out": nrm(ks[11], (N_CONV_LAYERS, E_CONV, D_MODEL), E_CONV ** -0.5),
        "hgrn_w_in": nrm(ks[12], (N_HGRN_LAYERS, D_MODEL, 2 * HG_F + 2 * HG_I), D_MODEL ** -0.5),
        "hgrn_lower_bounds": 1.0 + nrm(ks[13], (DEPTH, HG_F), 0.1),
        "hgrn_onorm_g": 1.0 + nrm(ks[14], (N_HGRN_LAYERS, HG_I), 0.02),
        "hgrn_w_out": nrm(ks[15], (N_HGRN_LAYERS, HG_I, D_MODEL), HG_I ** -0.5),
        "final_norm_g": 1.0 + nrm(ks[16], (D_MODEL,), 0.02),
    }


def reference(x_prompt, x_sample, state_conv, state_hgrn, c_prompt, c_sample, norm_g, w_ada, b_ada,
              conv_w_in, conv_w, conv_w_out, hgrn_w_in, hgrn_lower_bounds, hgrn_onorm_g, hgrn_w_out,
              final_norm_g):
    weights = (norm_g, w_ada, b_ada, conv_w_in, conv_w, conv_w_out, hgrn_w_in,
               hgrn_lower_bounds, hgrn_onorm_g, hgrn_w_out, final_norm_g)
    zero_conv = jnp.zeros((N_CONV_LAYERS, x_prompt.shape[0], CONV_WIDTH - 1, E_CONV), x_prompt.dtype)
    zero_hgrn = jnp.zeros((N_HGRN_LAYERS, x_prompt.shape[0], HG_HEADS, HG_DK, HG_DV), jnp.float32)
    y_prompt, conv_p, hgrn_p = trunk(x_prompt, c_prompt, zero_conv, zero_hgrn, *weights)
    y_sample, conv_s, hgrn_s = trunk(x_sample, c_sample, state_conv, state_hgrn, *weights)
    return (y_prompt, y_sample, conv_p, hgrn_p, conv_s, hgrn_s)
```

```python
import numpy as np
from contextlib import ExitStack
import concourse.bass as bass
import concourse.mybir as mybir
from concourse.bass_utils import run_bass_kernel_spmd

F32 = mybir.dt.float32
BF16 = mybir.dt.bfloat16
AF = mybir.ActivationFunctionType
ALU = mybir.AluOpType

NCORES = 8
D = 1024
DEPTH = 4
SEQ = 2048
NSEQ_S = 16
TS_ = 4
NSAMP = NSEQ_S * TS_
GTOK = 1024
XCOLS = GTOK + NSAMP
EPS = 1e-6
OFF_NG, OFF_FG, OFF_BA, OFF_CW, OFF_LB, OFF_OG, NV = 0, 32, 40, 136, 184, 216, 232
C_SM32, C_SM4, C_AM32, C_AM2, C_AM4, C_RM, C_ID, NCF = 0, 512, 576, 704, 832, 896, 912, 1040
FW = 520

GROUPS = [
    [(0, 512, "p"), (512, 512, "p")],
    [(0, 512, "p"), (512, 512, "p"), (GTOK, NSAMP, "s")],
]


class _Op:
    __slots__ = ("eng", "fn", "deps", "signal", "sigval", "sem", "isdma", "pos", "fdeps")


class Sched:
    ENGS = ("pe", "act", "dve", "pool", "sp")

    def __init__(self):
        self.q = {e: [] for e in self.ENGS}
        self.lastw = {}
        self.rd_c = {}
        self.rd_d = {}
        self.dmacount = {}
        self.stores = []

    def _mk(self, eng, fn, isdma):
        o = _Op()
        o.eng, o.fn, o.deps, o.signal, o.sigval, o.sem, o.isdma = eng, fn, set(), False, 0, None, isdma
        o.pos = len(self.q[eng])
        return o

    def _track(self, o, r, w):
        d = o.deps
        for k in r:
            x = self.lastw.get(k)
            if x is not None:
                d.add(x)
        for k in w:
            x = self.lastw.get(k)
            if x is not None:
                d.add(x)
            for y in self.rd_c.get(k, {}).values():
                d.add(y)
            for y in self.rd_d.get(k, ()):
                d.add(y)
        for k in w:
            self.lastw[k] = o
            self.rd_c[k] = {}
            self.rd_d[k] = []
        for k in r:
            if o.isdma:
                self.rd_d.setdefault(k, []).append(o)
            else:
                self.rd_c.setdefault(k, {})[o.eng] = o
        d.discard(o)
        self.q[o.eng].append(o)

    @staticmethod
    def _norm(r, w):
        isps = lambda k: isinstance(k, tuple) and k[0] == "ps"
        w2 = [k[:2] if isps(k) else k for k in w] + [k[:2] for k in r if isps(k)]
        r2 = [k for k in r if not isps(k)]
        return r2, w2

    def op(self, eng, fn, r=(), w=()):
        o = self._mk(eng, fn, False)
        r, w = self._norm(r, w)
        self._track(o, r, w)
        return o

    def dma(self, eng, fn, r=(), w=(), sk=None, store=False):
        o = self._mk(eng, fn, True)
        c = self.dmacount.get(sk, 0) + 1
        self.dmacount[sk] = c
        o.sem = sk
        o.sigval = 16 * c
        self._track(o, r, w)
        if store:
            self.stores.append(o)
        return o

    def emit(self, nc, es):
        fin = self._mk("sp", lambda e: e.nop(), False)
        fin.deps = set(self.stores)
        self.q["sp"].append(fin)
        for e in self.ENGS:
            for o in self.q[e]:
                best = {}
                fd = []
                for dd in o.deps:
                    if dd.isdma:
                        fd.append(dd)
                        continue
                    if dd.eng == e:
                        if e == "pe" or o.isdma and False:
                            continue
                    b = best.get(dd.eng)
                    if b is None or dd.pos > b.pos:
                        best[dd.eng] = dd
                for dd in best.values():
                    dd.signal = True
                    fd.append(dd)
                o.fdeps = fd
        esem = {e: es.enter_context(nc.semaphore("c_" + e)) for e in self.ENGS}
        dsem = {}
        for i, k in enumerate(self.dmacount):
            dsem[k] = es.enter_context(nc.semaphore("d%d" % i))
        for e in self.ENGS:
            c = 0
            for o in self.q[e]:
                if o.isdma:
                    o.sem = dsem[o.sem]
                else:
                    o.sem = esem[e]
                    if o.signal:
                        c += 1
                        o.sigval = c
        block = es.enter_context(nc.Block())

        def run(ename):
            def body(eng):
                waited = {}
                for o in self.q[ename]:
                    for dd in o.fdeps:
                        if waited.get(dd.sem, 0) < dd.sigval:
                            eng.wait_ge(dd.sem, dd.sigval)
                            waited[dd.sem] = dd.sigval
                    ins = o.fn(eng)
                    if o.isdma:
                        ins.then_inc(o.sem, 16)
                    elif o.signal:
                        ins.then_inc(o.sem, 1)
            return body

        block.tensor(run("pe"))
        block.scalar(run("act"))
        block.vector(run("dve"))
        block.gpsimd(run("pool"))
        block.sync(run("sp"))


class Ring:
    def __init__(self, name, bufs):
        self.name, self.bufs, self.i = name, bufs, 0

    def next(self):
        i = self.i % len(self.bufs)
        self.i += 1
        return self.bufs[i], (self.name, i)


def build_nc():
    nc = bass.Bass("TRN2", target_bir_lowering=False)
    S = Sched()
    es = ExitStack()

    def din(name, shape):
        return nc.dram_tensor(name, shape, F32, kind="ExternalInput").ap()

    def dout(name, shape):
        return nc.dram_tensor(name, shape, F32, kind="ExternalOutput").ap()

    xin = din("xin", [128, 8, SEQ + NSAMP])
    cT_d = din("cT", [128, 8 * 17])
    vecs_d = din("vecs", [128, NV])
    cf_d = din("consts", [128, NCF])
    scv_d = din("scv", [128, 512])
    shg_d = din("shg", [2, NSEQ_S, 8, 128, 128])
    wada_d = din("w_ada", [DEPTH, 24, 128, 8 * 128])
    win_d = din("w_in", [DEPTH, D, 4 * D])
    wout_d = din("w_out", [DEPTH, D, D])
    yout = dout("yout", [128, 8, SEQ + NSAMP])
    convP = dout("convP", [128, 32])
    convS = dout("convS", [128, 512])
    hgP = dout("hgP", [2, 8, 128, 128])
    hgS = dout("hgS", [2, NSEQ_S, 8, 128, 128])

    def sb(name, shape, dt=F32):
        return es.enter_context(nc.sbuf_tensor("s_" + name, shape, dt))

    xT = sb("xT", [128, 8, XCOLS])
    hT = sb("hT", [128, 8, 512], BF16)
    sqy = sb("sqy", [128, 8, 512], BF16)
    NWIN, NWOUT, NWADA = 8, 8, 2
    win_s = [sb("win%d" % i, [128, 8, 512], BF16) for i in range(NWIN)]
    wout_s = [sb("wout%d" % i, [128, 1024], BF16) for i in range(NWOUT)]
    wada_s = [sb("wada%d" % i, [128, 8, 128], BF16) for i in range(NWADA)]
    fr = Ring("f", [sb("fr%d" % i, [128, FW]) for i in range(5)])
    br = Ring("b", [sb("br%d" % i, [128, 512], BF16) for i in range(7)])
    rstd_t = sb("rstd_t", [128, 512])
    qt = sb("qt", [128, 4, 512], BF16)
    kt = sb("kt", [128, 4, 512], BF16)
    qx = sb("qx", [128, 4, 512], BF16)
    D2 = sb("D2", [128, 8, 16])
    dq = sb("dq", [128, 16])
    khtm = sb("khtm", [128, 4, 4, 128], BF16)
    vtm = sb("vtm", [128, 4, 4, 128], BF16)
    szb = sb("szb", [128, 4, 512], BF16)
    osq = sb("osq", [128, 4, 128], BF16)
    Sst = sb("Sst", [128, 2, 8, 128])
    Sbf = sb("Sbf", [128, 4, 128], BF16)
    stg = [sb("stg%d" % i, [128, 4, 128]) for i in range(2)]
    ustate = sb("ustate", [128, 32])
    scvt = [sb("scvt%d" % i, [128, 32]) for i in range(2)]
    usmt = [sb("usmt%d" % i, [128, 32]) for i in range(2)]
    S2 = sb("S2", [128, 4, 128])
    cf = sb("cf", [128, C_ID])
    vecs = sb("vecs", [128, NV])
    mod = sb("mod", [128, DEPTH, 24, 17])
    cT = sb("cT", [128, 8 * 17])
    scT = sb("scT", [128, 8, 17], BF16)
    ident = sb("ident", [128, 128], BF16)
    ones = sb("ones", [128, 128], BF16)
    dch = sb("dch", [128, 8, 16])
    lbw = sb("lbw", [128, 128])
    psb = [es.enter_context(nc.psum_tensor("ps%d" % i, [128, 512], F32)) for i in range(8)]

    class PRing:
        def __init__(self, banks):
            self.banks, self.i = banks, 0

        def next(self):
            b = self.banks[self.i % len(self.banks)]
            self.i += 1
            return psb[b], ("ps", b)

    proj = PRing([0, 1, 2])
    tpr = PRing([7])

    class SubRing:
        def __init__(self, bank):
            self.bank, self.i = bank, 0

        def next(self):
            s = self.i % 4
            self.i += 1
            return psb[self.bank][:, s * 128:(s + 1) * 128], ("ps", self.bank, s)

    aring = SubRing(5)
    kvring = SubRing(6)

    def MM(out, lhsT, rhs, st, sp, r, w, **kw):
        S.op("pe", lambda e: e.matmul(out, lhsT, rhs, start=st, stop=sp, **kw), r, w)

    def ACT(out, in_, func, r, w, bias=0.0, scale=1.0):
        S.op("act", lambda e: e.activation(out, in_, func, bias=bias, scale=scale), r, w)

    def TT(eng, out, a, b, op, r, w):
        S.op(eng, lambda e: e.tensor_tensor(out, a, b, op), r, w)

    def TSC(eng, out, a, s1, s2, op0, op1, r, w):
        S.op(eng, lambda e: e.tensor_scalar(out, a, s1, s2, op0, op1), r, w)

    def STT(out, in0, sc, in1, op0, op1, r, w):
        S.op("dve", lambda e: e.scalar_tensor_tensor(out, in0, sc, in1, op0, op1), r, w)

    def CP(eng, out, in_, r, w):
        S.op(eng, lambda e: e.tensor_copy(out, in_), r, w)

    def DMA(q, out, in_, r, w, sk, store=False):
        S.dma(q, lambda e: e.dma_start(out=out, in_=in_), r, w, sk, store)

    class Stream:
        def __init__(self, name, slots, total, src, extra=None):
            self.name, self.slots, self.total, self.src, self.nl = name, slots, total, src, 0
            self.extra = extra or (lambda s_: [])

        def load_next(self):
            if self.nl >= self.total:
                return
            i = self.nl
            self.nl += 1
            s = i % len(self.slots)
            DMA("pool", self.slots[s][:], self.src(i), [], [(self.name, s)] + self.extra(s), (self.name, s))

        def slot(self, i):
            s = i % len(self.slots)
            return self.slots[s], (self.name, s)

    def win_src(i):
        l, j = (i // 8) % DEPTH, i % 8
        return win_d[l].rearrange("(k p) e -> p k e", p=128)[:, :, j * 512:(j + 1) * 512]

    def wout_src(i):
        l, j = (i // 8) % DEPTH, i % 8
        return wout_d[l][j * 128:(j + 1) * 128, :]

    def wada_src(i):
        l, q = i // 24, i % 24
        if l == 0:
            q = (list(range(8, 16)) + list(range(0, 8)) + list(range(16, 24)))[q]
        return wada_d[l, q].rearrange("p (k e) -> p k e", e=128)

    winS = Stream("win", win_s, 2 * DEPTH * 8, win_src)
    woutS = Stream("wout", wout_s, 2 * DEPTH * 8, wout_src)
    wadaS = Stream("wada", wada_s, DEPTH * 24, wada_src)
    a0_slots, a0_keys = [], []
    for t_, nm in ((qt, "qt"), (kt, "kt"), (szb, "szb")):
        for hlf in range(2):
            a0_slots.append(t_[:, 2 * hlf:2 * hlf + 2, :].rearrange("p a (b c) -> p (a b) c", c=128))
            a0_keys.append([(nm, 2 * hlf), (nm, 2 * hlf + 1)])
    for hlf in range(2):
        a0_slots.append(khtm[:, 2 * hlf:2 * hlf + 2, :, :].rearrange("p a b c -> p (a b) c"))
        a0_keys.append([("khtm", x) for x in range(4)])
        a0_slots.append(vtm[:, 2 * hlf:2 * hlf + 2, :, :].rearrange("p a b c -> p (a b) c"))
        a0_keys.append([("vtm", 2 * hlf), ("vtm", 2 * hlf + 1)])
    wada0S = Stream("wada0", a0_slots, 24, wada_src, extra=lambda s_: a0_keys[s_])
    wadaS.nl = 24

    DMA("sp", cT[:], cT_d, [], ["cT"], "ld_cT")
    DMA("sp", vecs[:], vecs_d, [], ["vecs"], "ld_vecs")
    DMA("sp", cf[:], cf_d[:, 0:C_ID], [], ["cf"], "ld_cf")
    DMA("pool", ident[:], cf_d[:, C_ID:C_ID + 128], [], ["ident"], "ld_id")
    for _ in range(len(a0_slots)):
        wada0S.load_next()
    S.op("pool", lambda e: e.memset(ones[:], 1.0), [], ["ones"])
    S.op("pool", lambda e: e.memset(dq[:], 1.0), [], ["dq"])
    S.op("pool", lambda e: e.memset(ustate[:], 0.0), [], ["ustate%d_%d" % (a, b) for a in range(2) for b in range(8)])
    S.op("pool", lambda e: e.memset(Sst[:], 0.0), [], [("Sst", a, b) for a in range(2) for b in range(8)])
    ACT(scT[:].rearrange("p a b -> p (a b)"), cT[:], AF.Silu, ["cT"], ["scT"])
    lbin = vecs[:, OFF_LB:OFF_LB + 32]
    mx, ex, sm = lbw[:, 0:8], lbw[:, 8:40], lbw[:, 40:48]
    TT("dve", mx, lbin[:, 0:8], lbin[:, 8:16], ALU.max, ["vecs"], ["lb_mx"])
    TT("dve", mx, mx, lbin[:, 16:24], ALU.max, ["vecs", "lb_mx"], ["lb_mx"])
    TT("dve", mx, mx, lbin[:, 24:32], ALU.max, ["vecs", "lb_mx"], ["lb_mx"])
    TT("dve", ex.rearrange("p (l h) -> p l h", h=8), lbin.rearrange("p (l h) -> p l h", h=8),
       mx.unsqueeze(1).to_broadcast([128, 4, 8]), ALU.subtract, ["vecs", "lb_mx"], ["lb_ex"])
    ACT(ex, ex, AF.Exp, ["lb_ex"], ["lb_ex"])
    TT("dve", sm, ex[:, 0:8], ex[:, 8:16], ALU.add, ["lb_ex"], ["lb_sm"])
    TT("dve", sm, sm, ex[:, 16:24], ALU.add, ["lb_ex", "lb_sm"], ["lb_sm"])
    TT("dve", sm, sm, ex[:, 24:32], ALU.add, ["lb_ex", "lb_sm"], ["lb_sm"])
    S.op("dve", lambda e: e.reciprocal(sm, sm), ["lb_sm"], ["lb_sm"])
    TT("dve", ex.rearrange("p (l h) -> p l h", h=8), ex.rearrange("p (l h) -> p l h", h=8),
       sm.unsqueeze(1).to_broadcast([128, 4, 8]), ALU.mult, ["lb_ex", "lb_sm"], ["lb_ex"])
    LB, OML, NOML = 48, 64, 80
    CP("dve", lbw[:, LB:LB + 8], ex[:, 8:16], ["lb_ex"], ["lb0"])
    TT("dve", lbw[:, LB + 8:LB + 16], ex[:, 8:16], ex[:, 16:24], ALU.add, ["lb_ex"], ["lb1"])
    TT("dve", lbw[:, LB + 8:LB + 16], lbw[:, LB + 8:LB + 16], ex[:, 24:32], ALU.add, ["lb_ex", "lb1"], ["lb1"])
    TSC("dve", lbw[:, OML:OML + 16], lbw[:, LB:LB + 16], -1.0, 1.0, ALU.mult, ALU.add, ["lb0", "lb1"], ["oml"])
    TSC("dve", lbw[:, NOML:NOML + 16], lbw[:, LB:LB + 16], 1.0, -1.0, ALU.mult, ALU.add, ["lb0", "lb1"], ["noml"])
    LBH, HOML, NHOML = 96, 64, 80
    S.op("dve", lambda e: e.scalar_tensor_tensor(lbw[:, LBH:LBH + 16], lbw[:, OML:OML + 16], 0.5, lbw[:, LB:LB + 16],
                                                  ALU.mult, ALU.add), ["oml", "lb0", "lb1"], ["lbh"])
    TSC("dve", lbw[:, HOML:HOML + 16], lbw[:, OML:OML + 16], 0.5, None, ALU.mult, ALU.bypass, ["oml", "lbh"], ["oml"])
    TSC("dve", lbw[:, NHOML:NHOML + 16], lbw[:, NOML:NOML + 16], 0.5, None, ALU.mult, ALU.bypass, ["noml"], ["noml"])
    S.op("dve", lambda e: e.engine_nop(), ["oml", "noml", "lbh"], ["lbc"])

    ORDER0 = list(range(8, 16)) + list(range(0, 8)) + list(range(16, 24))

    def ada_part(l, positions, fin):
        strm = wada0S if l == 0 else wadaS
        for p_ in positions:
            q = ORDER0[p_] if l == 0 else p_
            i = l * 24 + p_
            slot, sk = strm.slot(i)
            bank, bk = proj.next()
            for k in range(8):
                MM(bank[:, 0:17], slot[:, k, :], scT[:, k, :], k == 0, k == 7,
                   [sk, "scT"] + strm.extra(i % len(strm.slots)), [bk])
            strm.load_next()
            TSC("dve", mod[:, l, q, :], bank[:, 0:17], vecs[:, OFF_BA + 24 * l + q:OFF_BA + 24 * l + q + 1], None,
                ALU.add, ALU.bypass, [bk, "vecs"], [("mod", l, q // 8)])
            yield
        if fin:
            TSC("dve", mod[:, l, 8:16, :], mod[:, l, 8:16, :], 1.0, None, ALU.add, ALU.bypass,
                [("mod", l, 1)], [("mod", l, 1)])
            TT("dve", mod[:, l, 8:16, :], mod[:, l, 8:16, :],
               vecs[:, OFF_NG + 8 * l:OFF_NG + 8 * (l + 1)].unsqueeze(2).to_broadcast([128, 8, 17]),
               ALU.mult, [("mod", l, 1), "vecs"], [("mod", l, 1)])
            yield

    def ada_steps(l):
        if l == 0:
            return iter(())
        return ada_part(l, range(24), True)

    def _ada0():
        yield from ada_part(0, range(0, 16), True)
        yield from ada_part(0, range(16, 24), False)

    for i_, _ in enumerate(_ada0()):
        if i_ % 3 == 0 and i_ // 3 < NWIN:
            winS.load_next()
    while winS.nl < NWIN:
        winS.load_next()
    for _ in range(NWOUT):
        woutS.load_next()
    for _ in range(NWADA):
        wadaS.load_next()
    ada_gen = [None]

    def ada_tick():
        if ada_gen[0] is not None:
            try:
                next(ada_gen[0])
            except StopIteration:
                ada_gen[0] = None

    def xkeys(ti):
        return [("x", c, ti) for c in range(8)]

    def stage_norm(l, ti, c0, n, kind, g_vec_off=None, final=False):
        bank, bk = proj.next()
        for c in range(8):
            sq, sqk = br.next()
            ACT(sq[:, 0:n], xT[:, c, c0:c0 + n], AF.Square, [("x", c, ti)], [sqk])
            MM(bank[:, 0:n], ones[:, :], sq[:, 0:n], c == 0, c == 7, ["ones", sqk], [bk])
        rstd, rk = rstd_t, "rstd"
        ACT(rstd[:, 0:n], bank[:, 0:n], AF.Ln, [bk], [rk], bias=EPS, scale=1.0 / D)
        ACT(rstd[:, 0:n], rstd[:, 0:n], AF.Exp, [rk], [rk], scale=-0.5)
        return rstd, rk

    def stage_h(l, ti, c0, n, kind):
        rstd, rk = stage_norm(l, ti, c0, n, kind)
        if kind == "p":
            for c in range(8):
                tmp, tk = fr.next()
                STT(tmp[:, 0:n], xT[:, c, c0:c0 + n], mod[:, l, 8 + c, 0:1], rstd[:, 0:n], ALU.mult, ALU.mult,
                    [("x", c, ti), ("mod", l, 1), rk], [tk])
                ACT(hT[:, c, 0:n], tmp[:, 0:n], AF.Identity, [tk, ("mod", l, 0)], [("h", c)],
                    bias=mod[:, l, c, 0:1], scale=1.0)
        else:
            tmp, tk = fr.next()
            tv = tmp[:, 0:512].rearrange("p (c t) -> p c t", c=8)
            TT("dve", tv, xT[:, :, c0:c0 + n], rstd[:, 0:n].unsqueeze(1).to_broadcast([128, 8, n]), ALU.mult,
               xkeys(ti) + [rk], [tk])
            tv4 = tmp[:, 0:512].rearrange("p (c s t) -> p c s t", c=8, t=TS_)
            TT("dve", tv4, tv4, mod[:, l, 8:16, 1:17].unsqueeze(3).to_broadcast([128, 8, NSEQ_S, TS_]), ALU.mult,
               [tk, ("mod", l, 1)], [tk])
            TT("dve", hT[:, :, 0:n].rearrange("p c (s t) -> p c s t", t=TS_), tv4,
               mod[:, l, 0:8, 1:17].unsqueeze(3).to_broadcast([128, 8, NSEQ_S, TS_]), ALU.add,
               [tk, ("mod", l, 0)], [("h", c) for c in range(8)])

    def stage_out(g, l, ti, c0, n, kind):
        base = (g * DEPTH + l) * 8
        for m in range(8):
            bank, bk = proj.next()
            for j in range(8):
                slot, sk = woutS.slot(base + j)
                MM(bank[:, 0:n], slot[:, m * 128:(m + 1) * 128], sqy[:, j, 0:n], j == 0, j == 7,
                   [sk, ("sqy", j)], [bk])
            if kind == "p":
                STT(xT[:, m, c0:c0 + n], bank[:, 0:n], mod[:, l, 16 + m, 0:1], xT[:, m, c0:c0 + n],
                    ALU.mult, ALU.add, [bk, ("mod", l, 2), ("x", m, ti)], [("x", m, ti)])
            else:
                tmp, tk = fr.next()
                t3 = tmp[:, 0:n].rearrange("p (s t) -> p s t", t=TS_)
                TT("dve", t3, bank[:, 0:n].rearrange("p (s t) -> p s t", t=TS_),
                   mod[:, l, 16 + m, 1:17].unsqueeze(2).to_broadcast([128, NSEQ_S, TS_]), ALU.mult,
                   [bk, ("mod", l, 2)], [tk])
                TT("dve", xT[:, m, c0:c0 + n], xT[:, m, c0:c0 + n], tmp[:, 0:n], ALU.add,
                   [tk, ("x", m, ti)], [("x", m, ti)])

    def conv_tile(g, l, ti, c0, n, kind, last, pre_h=False, next_h=None):
        jl = l // 2
        base = (g * DEPTH + l) * 8
        proj.banks = [0, 1, 2, 3, 4, 5, 6, 7]
        if not pre_h:
            stage_h(l, ti, c0, n, kind)
        hk = [("h", c) for c in range(8)]
        for j in range(8):
            slot, sk = winS.slot(base + j)
            def grp(off):
                bank, bk = proj.next()
                for k in range(8):
                    MM(bank[:, 0:n], slot[:, k, off:off + 128], hT[:, k, 0:n], k == 0, k == 7, [sk] + hk, [bk])
                return bank, bk
            pv, pvk = grp(256)
            vs, vk = fr.next()
            ACT(vs[:, 0:n], pv[:, 0:n], AF.Copy, [pvk], [vk])
            pc, pck = grp(128)
            ue, uk = fr.next()
            cw = [vecs[:, OFF_CW + (jl * 3 + t) * 8 + j:OFF_CW + (jl * 3 + t) * 8 + j + 1] for t in range(3)]
            t1, t1k = fr.next()
            if kind == "p":
                usl = ustate[:, (jl * 8 + j) * 2:(jl * 8 + j) * 2 + 2]
                CP("pool", ue[:, 0:2], usl, ["ustate%d_%d" % (jl, j)], [uk])
                TT("dve", ue[:, 2:n + 2], pc[:, 0:n], vs[:, 0:n], ALU.mult, [pck, vk, uk], [uk])
                CP("pool", usl, ue[:, n:n + 2], [uk], ["ustate%d_%d" % (jl, j)])
                u0, u1, u2 = ue[:, 0:n], ue[:, 1:n + 1], ue[:, 2:n + 2]
                t1v = t1[:, 0:n]
            else:
                ue3 = ue[:, 0:NSEQ_S * 6].rearrange("p (s t) -> p s t", t=6)
                cs = (jl * 8 + j) * 32
                si = j % 2
                DMA("sp", scvt[si][:], scv_d[:, cs:cs + 32], [], [("scvt", si)], ("scvt", si))
                CP("pool", ue3[:, :, 0:2], scvt[si][:].rearrange("p (s t) -> p s t", t=2), [("scvt", si)], [uk])
                TT("dve", ue3[:, :, 2:6], pc[:, 0:n].rearrange("p (s t) -> p s t", t=TS_),
                   vs[:, 0:n].rearrange("p (s t) -> p s t", t=TS_), ALU.mult, [pck, vk, uk], [uk])
                CP("pool", usmt[si][:].rearrange("p (s t) -> p s t", t=2), ue3[:, :, 4:6], [uk], [("usmt", si)])
                DMA("sp", convS[:, cs:cs + 32], usmt[si][:], [("usmt", si)], [], ("usmt_st", si), store=True)
                u0, u1, u2 = ue3[:, :, 0:4], ue3[:, :, 1:5], ue3[:, :, 2:6]
                t1v = t1[:, 0:n].rearrange("p (s t) -> p s t", t=TS_)
            ACT(t1v, u0, AF.Identity, [uk, "vecs"], [t1k], scale=cw[0])
            STT(t1v, u1, cw[1], t1v, ALU.mult, ALU.add, [uk, t1k, "vecs"], [t1k])
            STT(t1v, u2, cw[2], t1v, ALU.mult, ALU.add, [uk, t1k, "vecs"], [t1k])
            ada_tick()
            pz, pzk = grp(384)
            sz, szk = fr.next()
            ACT(sz[:, 0:n], pz[:, 0:n], AF.Silu, [pzk], [szk])
            pbb, pbk = grp(0)
            TT("dve", t1[:, 0:n], pbb[:, 0:n], t1[:, 0:n], ALU.mult, [pbk, t1k], [t1k])
            TT("dve", sqy[:, j, 0:n], t1[:, 0:n], sz[:, 0:n], ALU.mult, [t1k, szk], [("sqy", j)])
            if last:
                winS.load_next()
            ada_tick()
        if next_h is not None:
            next_h()
        stage_out(g, l, ti, c0, n, kind)
        if last:
            for _ in range(8):
                woutS.load_next()

    def hgrn_tile(g, l, ti, c0, n, kind, last, pre_h=False, next_h=None):
        jl = l // 2
        base = (g * DEPTH + l) * 8
        proj.banks = [0, 1, 2]
        if not pre_h:
            stage_h(l, ti, c0, n, kind)
        hk = [("h", c) for c in range(8)]
        if kind == "p":
            L, nb, nblk, cpb = 32, 128, n // 128, 4
            smask = cf[:, C_SM32:C_SM32 + n]
            amask = cf[:, C_AM32:C_AM32 + 128]
        else:
            L, nb, nblk, cpb = TS_, NSAMP, 1, NSEQ_S
            smask = cf[:, C_SM4:C_SM4 + n]
            amask = cf[:, C_AM4:C_AM4 + NSAMP]
        nch = n // L
        Skh = lambda h: ("Sst", jl, h)
        for hf in range(2):
            hs = list(range(4 * hf, 4 * hf + 4))
            proj.banks = [0, 1, 2, 4, 6, 7]

            def grp(slot, sk, off):
                bank, bk = proj.next()
                for k in range(8):
                    MM(bank[:, 0:n], slot[:, k, off:off + 128], hT[:, k, 0:n], k == 0, k == 7, [sk] + hk, [bk])
                return bank, bk
            def P1h(hh):
                h = hs[hh]
                slot, sk = winS.slot(base + h)
                bq, bqk = grp(slot, sk, 0)
                ACT(qt[:, hh, 0:n], bq[:, 0:n], AF.Silu, [bqk], [("qt", hh)])
                bf_, bfk = grp(slot, sk, 128)
                th, thk = fr.bufs[hh], ("f", hh)
                ACT(th[:, 0:n], bf_[:, 0:n], AF.Tanh, [bfk], [thk], scale=0.5)
                bz, bzk = grp(slot, sk, 384)
                ACT(szb[:, hh, 0:n], bz[:, 0:n], AF.Silu, [bzk], [("szb", hh)])
                ogc = OFF_OG + jl * 8 + h
                TSC("pool", szb[:, hh, 0:n], szb[:, hh, 0:n], vecs[:, ogc:ogc + 1], 1.0, ALU.mult, ALU.mult,
                    [("szb", hh), "vecs"], [("szb", hh)])
                ada_tick()
            pend = []

            def flush_tr():
                for (khT, khk, hh) in pend:
                    tb, tbk = proj.next()
                    tbb = tb[:, :].bitcast(BF16)
                    for blk in range(nblk):
                        S.op("pe", lambda e, o=tbb[0:nb, blk * 128:(blk + 1) * 128], i=khT[:, blk * nb:(blk + 1) * nb]:
                             e.transpose(o, i, ident[:, :]), [khk, "ident"], [tbk])
                    ACT(khtm[0:nb, 0:nblk, hh, :], tbb[0:nb, 0:nblk * 128].rearrange("p (b k) -> p b k", k=128),
                        AF.Copy, [tbk], [("khtm", hh)])
                del pend[:]
            pend_v = []

            def flush_v():
                for (bv, bvk, blk) in pend_v:
                    ACT(vtm[0:nb, blk, :, :].rearrange("p h v -> p (h v)"), bv[0:nb, 0:512], AF.Copy, [bvk],
                        [("vtm", blk)])
                del pend_v[:]

            def vproj(blk):
                bv, bvk = proj.next()
                for hh2, h2 in enumerate(hs):
                    slot, sk = winS.slot(base + h2)
                    for k in range(8):
                        MM(bv[0:nb, hh2 * 128:(hh2 + 1) * 128], hT[:, k, blk * nb:(blk + 1) * nb], slot[:, k, 256:384],
                           k == 0, k == 7, [sk] + hk, [bvk])
                pend_v.append((bv, bvk, blk))
            hst = {}
            G, Gk = fr.bufs[4], ("f", 4)

            def stA(hh):
                h = hs[hh]
                col = jl * 8 + h
                th, thk = fr.bufs[hh], ("f", hh)
                kk, kkk = br.next()
                TSC("dve", kk[:, 0:n], th[:, 0:n], lbw[:, NHOML + col:NHOML + col + 1],
                    lbw[:, HOML + col:HOML + col + 1], ALU.mult, ALU.add, [thk, "lbc"], [kkk])
                TSC("dve", th[:, 0:n], th[:, 0:n], lbw[:, HOML + col:HOML + col + 1],
                    lbw[:, LBH + col:LBH + col + 1], ALU.mult, ALU.add, [thk, "lbc"], [thk])
                hst[hh] = {"kk": (kk, kkk)}

            def stB(hh):
                th, thk = fr.bufs[hh], ("f", hh)
                S.op("dve", lambda e, G=G, lf=th, smask=smask: e.tensor_tensor_scan(
                    G[:, 0:n], smask, lf[:, 0:n], 0.0, ALU.max, ALU.mult), [thk, "cf"], [Gk])

            def stC(hh):
                h = hs[hh]
                eG, eGk = br.next()
                ACT(eG[:, 0:n], G[:, 0:n], AF.Copy, [Gk], [eGk])
                enG, enGk = br.next()
                def _rc(e, o=enG[:, 0:n], i=G[:, 0:n]):
                    with nc.allow_low_precision("1/decay feeds a bf16 matmul operand"):
                        return e.reciprocal(o, i)
                S.op("dve", _rc, [Gk], [enGk])
                ACT(dch[:, h, 0:nch], G[:, L - 1:n:L], AF.Copy, [Gk], [("dch", h)])
                hst[hh]["eG"] = (eG, eGk)
                hst[hh]["enG"] = (enG, enGk)

            def stD(hh):
                kk, kkk = hst[hh]["kk"]
                eG, eGk = hst[hh]["eG"]
                enG, enGk = hst[hh]["enG"]
                TT("dve", qt[:, hh, 0:n], qt[:, hh, 0:n], eG[:, 0:n], ALU.mult, [("qt", hh), eGk], [("qt", hh)])
                TT("dve", kt[:, hh, 0:n], kk[:, 0:n], enG[:, 0:n], ALU.mult, [kkk, enGk], [("kt", hh)])

            def stE(hh):
                h = hs[hh]
                if kind == "p":
                    CP("pool", D2[:, h, 0:nch], dch[:, h, 0:nch], [("dch", h)], [("D2", h)])
                    TT("pool", D2[:, h, 0:nch:2], D2[:, h, 0:nch:2], dch[:, h, 1:nch:2], ALU.mult,
                       [("D2", h), ("dch", h)], [("D2", h)])
                    CP("pool", dq[:, 1:nch:2], dch[:, h, 0:nch:2], [("dch", h)], ["dq"])
                    dsel, dselk = D2, ("D2", h)
                else:
                    dsel, dselk = dch, ("dch", h)
                khT, khk = br.next()
                TT("pool", khT[:, 0:n].rearrange("p (c t) -> p c t", t=L),
                   kt[:, hh, 0:n].rearrange("p (c t) -> p c t", t=L),
                   dsel[:, h, 0:nch].unsqueeze(2).to_broadcast([128, nch, L]), ALU.mult,
                   [("kt", hh), dselk], [khk])
                if kind == "p":
                    TT("pool", qx[:, hh, 0:n].rearrange("p (c t) -> p c t", t=L),
                       qt[:, hh, 0:n].rearrange("p (c t) -> p c t", t=L),
                       dq[:, 0:nch].unsqueeze(2).to_broadcast([128, nch, L]), ALU.mult,
                       [("qt", hh), "dq"], [("qx", hh)])
                pend.append((khT, khk, hh))

            P1h(0)
            stA(0)
            stB(0)
            for hh, h in enumerate(hs):
                if hh + 1 < 4:
                    P1h(hh + 1)
                stC(hh)
                stD(hh)
                flush_tr()
                stE(hh)
                if hh < nblk:
                    vproj(hh)
                if hh + 1 < 4:
                    stA(hh + 1)
                    stB(hh + 1)
                flush_v()
            flush_tr()
            flush_v()
            if last:
                for _ in range(4):
                    winS.load_next()
            proj.banks = [0, 1, 2]
            if kind == "p":
                ACT(Sbf[:, 0:4, :], Sst[:, jl, 4 * hf:4 * hf + 4, :], AF.Copy, [Skh(h) for h in hs],
                    [("Sbf", h) for h in hs])
            else:
                stgA = [stg[0][:], stg[1][:],
                        qx[:, 0:2, :].bitcast(F32).rearrange("p a (b v) -> p (a b) v", v=128),
                        qx[:, 2:4, :].bitcast(F32).rearrange("p a (b v) -> p (a b) v", v=128)]
                NSTG = 4

                def stgk(i):
                    ks = [("stg", i, x) for x in range(4)]
                    if i >= 2:
                        ks += [("qx", 2 * (i - 2)), ("qx", 2 * (i - 2) + 1)]
                    return ks

                def ld(s):
                    i = s % NSTG
                    DMA("sp", stgA[i], shg_d[jl, s, 4 * hf:4 * hf + 4].rearrange("h k v -> k h v"),
                        [], stgk(i), ("stg", i))
                for s_ in range(NSTG):
                    ld(s_)

            nst = {}

            pob = (lambda blk: 3 + (blk % 3)) if kind == "p" else (lambda blk: 3 + (blk % 2))

            nst = {}

            def norm_sq(blk):
                ob = pob(blk)
                po, pok = psb[ob], ("ps", ob)
                for hh in range(4):
                    ACT(osq[:, hh, 0:nb], po[:, hh * 128:hh * 128 + nb], AF.Square, [pok], [("osq", hh)])

            def norm_nm(blk):
                br_, brk = proj.next()
                for hh in range(4):
                    MM(br_[:, hh * 128:hh * 128 + nb], ones[:, :], osq[:, hh, 0:nb], True, True,
                       ["ones", ("osq", hh)], [brk])
                nst[blk] = (br_, brk)

            def norm_p1(blk):
                norm_sq(blk)
                norm_nm(blk)

            def norm_p2(blk):
                br_, brk = nst[blk]
                brv = br_[:, :].rearrange("p (h t) -> p h t", h=4)[:, :, 0:nb]
                lnr, lnk = fr.bufs[4], ("f", 4)
                lnrv = lnr[:, 0:512].rearrange("p (h t) -> p h t", h=4)[:, :, 0:nb]
                ACT(lnrv, brv, AF.Ln, [brk], [lnk], bias=EPS, scale=1.0 / 128)
                ACT(lnrv, lnrv, AF.Exp, [lnk], [lnk], scale=-0.5)

            def norm_p3(blk):
                ob = pob(blk)
                po, pok = psb[ob], ("ps", ob)
                bc0 = blk * nb
                lnr, lnk = fr.bufs[4], ("f", 4)
                t1, t1k = fr.bufs[0], ("f", 0)
                t1v = t1[:, 0:512].rearrange("p (h t) -> p h t", h=4)[:, :, 0:nb]
                lnrv = lnr[:, 0:512].rearrange("p (h t) -> p h t", h=4)[:, :, 0:nb]
                pov = po[:, 0:512].rearrange("p (h t) -> p h t", h=4)[:, :, 0:nb]
                TT("dve", t1v, pov, lnrv, ALU.mult, [pok, lnk], [t1k])
                TT("pool", sqy[:, 4 * hf:4 * hf + 4, bc0:bc0 + nb], t1v, szb[:, :, bc0:bc0 + nb], ALU.mult,
                   [t1k] + [("szb", x) for x in range(4)], [("sqy", h) for h in hs])

            def emit_A(blk):
                bc0 = blk * nb
                if kind == "p":
                    res = []
                    for rnd in range(2):
                        pa, pak = proj.next()
                        for hl in range(2):
                            hh = 2 * rnd + hl
                            MM(pa[:, (2 * hl) * 128:(2 * hl + 1) * 128], kt[:, hh, bc0:bc0 + 128],
                               qt[:, hh, bc0:bc0 + 128], True, True, [("kt", hh), ("qt", hh)], [pak])
                            MM(pa[:, (2 * hl + 1) * 128:(2 * hl + 2) * 128], kt[:, hh, bc0:bc0 + 128],
                               qx[:, hh, bc0:bc0 + 128], True, True, [("kt", hh), ("qx", hh)], [pak])
                        atm, atk = br.next()
                        TT("dve", atm[:, :].rearrange("p (h t) -> p h t", h=2),
                           pa[:, :].rearrange("p (h t) -> p h t", h=2),
                           cf[:, C_AM32:C_AM32 + 256].unsqueeze(1).to_broadcast([128, 2, 256]), ALU.mult,
                           [pak, "cf"], [atk])
                        res.append((atm, atk))
                    return res
                pa, pak = psb[5], ("ps", 5)
                for hh, h in enumerate(hs):
                    MM(pa[0:nb, hh * 128:hh * 128 + nb], kt[:, hh, bc0:bc0 + nb], qt[:, hh, bc0:bc0 + nb], True, True,
                       [("kt", hh), ("qt", hh)], [pak])
                atm, atk = br.next()
                TT("dve", atm[0:nb, :].rearrange("p (h t) -> p h t", h=4)[:, :, 0:nb],
                   pa[0:nb, :].rearrange("p (h t) -> p h t", h=4)[:, :, 0:nb],
                   amask[0:nb, 0:nb].unsqueeze(1).to_broadcast([nb, 4, nb]), ALU.mult, [pak, "cf"], [atk])
                return [(atm, atk)]

            def emit_oi(blk, res):
                ob = pob(blk)
                po, pok = psb[ob], ("ps", ob)
                for hh, h in enumerate(hs):
                    if kind == "p":
                        atm, atk = res[hh // 2]
                        hl = hh % 2
                        MM(po[:, hh * 128:hh * 128 + 128], vtm[:, blk, hh, :], atm[:, (2 * hl) * 128:(2 * hl + 1) * 128],
                           hh == 0, False, [("vtm", blk), atk], [pok], skip_group_check=True)
                        MM(po[:, hh * 128:hh * 128 + 128], vtm[:, blk, hh, :],
                           atm[:, (2 * hl + 1) * 128:(2 * hl + 2) * 128],
                           False, False, [("vtm", blk), atk], [pok], skip_group_check=True)
                    else:
                        atm, atk = res[0]
                        MM(po[:, hh * 128:hh * 128 + nb], vtm[0:nb, blk, hh, :], atm[0:nb, hh * 128:hh * 128 + nb],
                           hh == 0, False, [("vtm", blk), atk], [pok], skip_group_check=True)

            nxtA = emit_A(0)
            emit_oi(0, nxtA)
            if kind == "p":
                steps = [(blk, c) for blk in range(nblk) for c in range(2)]

                def emit_KV(gs):
                    blk, c = steps[gs]
                    kvb = 6 + gs % 2
                    pk, pkk = psb[kvb], ("ps", kvb)
                    for hh, h in enumerate(hs):
                        MM(pk[:, hh * 128:(hh + 1) * 128], khtm[c * 64:(c + 1) * 64, blk, hh, :],
                           vtm[c * 64:(c + 1) * 64, blk, hh, :],
                           True, True, [("khtm", hh), ("vtm", blk)], [pkk], tile_position=(c * 64, 0))
                emit_KV(0)
                assert len(steps) % 2 == 0
                for gs, (blk, c) in enumerate(steps):
                    ob = pob(blk)
                    po, pok = psb[ob], ("ps", ob)
                    bc0 = blk * nb
                    ci = blk * cpb + 2 * c
                    if gs + 1 < len(steps):
                        emit_KV(gs + 1)
                    for hh, h in enumerate(hs):
                        MM(po[:, hh * 128 + c * 64:hh * 128 + (c + 1) * 64], Sbf[:, hh, :],
                           qx[:, hh, bc0 + c * 64:bc0 + (c + 1) * 64],
                           False, c == 1, [("Sbf", h), ("qx", hh)], [pok], skip_group_check=True)
                    kvb = 6 + gs % 2
                    pk, pkk = psb[kvb], ("ps", kvb)
                    for hh, h in enumerate(hs):
                        bufs = [(Sst[:, jl, h, :], Skh(h)), (S2[:, hh, :], ("S2", hh))]
                        (src, srck), (dst, dstk) = bufs[gs % 2], bufs[(gs + 1) % 2]
                        STT(dst, src, D2[:, h, ci:ci + 1], pk[:, hh * 128:(hh + 1) * 128],
                            ALU.mult, ALU.add, [srck, ("D2", h), pkk], [dstk])
                    if gs % 2 == 0:
                        ACT(Sbf[:, 0:4, :], S2[:, :, :], AF.Copy, [("S2", x) for x in range(4)],
                            [("Sbf", h) for h in hs])
                    else:
                        ACT(Sbf[:, 0:4, :], Sst[:, jl, 4 * hf:4 * hf + 4, :], AF.Copy,
                            [Skh(h) for h in hs], [("Sbf", h) for h in hs])
                    ada_tick()
                    if c == 0:
                        if blk > 0:
                            norm_sq(blk - 1)
                        if blk + 1 < nblk:
                            nxtA = emit_A(blk + 1)
                        if blk > 1:
                            norm_p3(blk - 2)
                    if c == 1:
                        if blk > 0:
                            norm_nm(blk - 1)
                            norm_p2(blk - 1)
                        if blk + 1 < nblk:
                            emit_oi(blk + 1, nxtA)
                if nblk > 1:
                    norm_p3(nblk - 2)
            else:
                po, pok = psb[pob(0)], ("ps", pob(0))
                for c in range(cpb):
                    i = c % NSTG
                    st_ = stgA[i]
                    sbs, sbsk = br.next()
                    ACT(sbs[:, 0:512], st_.rearrange("p h v -> p (h v)"), AF.Copy, stgk(i), [sbsk])
                    khm, khmk = br.next()
                    ACT(khm[0:nb, 0:512], khtm[0:nb, 0, :, :].rearrange("p h k -> p (h k)"), AF.Identity,
                        [("khtm", x) for x in range(4)] + ["cf"], [khmk], scale=cf[0:nb, C_RM + c:C_RM + c + 1])
                    kvb = 6 + c % 2
                    pk, pkk = psb[kvb], ("ps", kvb)
                    for hh, h in enumerate(hs):
                        MM(pk[:, hh * 128:(hh + 1) * 128], khm[0:nb, hh * 128:(hh + 1) * 128],
                           vtm[0:nb, 0, hh, :], True, True, [khmk, ("vtm", 0)], [pkk])
                    for hh, h in enumerate(hs):
                        MM(po[:, hh * 128 + c * L:hh * 128 + (c + 1) * L], sbs[:, hh * 128:(hh + 1) * 128],
                           qt[:, hh, c * L:(c + 1) * L],
                           False, c == cpb - 1, [sbsk, ("qt", hh)], [pok], skip_group_check=True)
                    for hh, h in enumerate(hs):
                        STT(st_[:, hh, :], st_[:, hh, :], dch[:, h, c:c + 1], pk[:, hh * 128:(hh + 1) * 128],
                            ALU.mult, ALU.add, stgk(i) + [("dch", h), pkk], stgk(i))
                    DMA("pool", hgS[jl, c, 4 * hf:4 * hf + 4].rearrange("h k v -> k h v"), st_,
                        stgk(i), [], ("stgst", i), store=True)
                    if c + NSTG < cpb:
                        ld(c + NSTG)
            norm_p1(nblk - 1)
            norm_p2(nblk - 1)
            norm_p3(nblk - 1)
        proj.banks = [0, 1, 2, 3, 4, 5, 6, 7]
        if next_h is not None:
            next_h()
        stage_out(g, l, ti, c0, n, kind)
        if last:
            for _ in range(8):
                woutS.load_next()

    def final_tile(g, ti, c0, n, kind):
        rstd, rk = stage_norm(DEPTH, ti, c0, n, kind)
        for c in range(8):
            STT(xT[:, c, c0:c0 + n], xT[:, c, c0:c0 + n], vecs[:, OFF_FG + c:OFF_FG + c + 1], rstd[:, 0:n],
                ALU.mult, ALU.mult, [("x", c, ti), "vecs", rk], [("x", c, ti)])
        oc0 = g * GTOK + c0 if kind == "p" else SEQ
        DMA("sp", yout[:, :, oc0:oc0 + n], xT[:, :, c0:c0 + n], xkeys(ti), [], ("st_y", ti), store=True)
        if g == 0 and kind == "p":
            DMA("sp", xT[:, :, c0:c0 + n], xin[:, :, GTOK + c0:GTOK + c0 + n], [],
                [("x", c, ti) for c in range(8)], ("ld_x", ti))

    if _DEBUG_STOP is None:
        for t_ in range(2):
            DMA("sp", xT[:, :, 512 * t_:512 * (t_ + 1)], xin[:, :, 512 * t_:512 * (t_ + 1)], [],
                [("x", c, t_) for c in range(8)], ("ld_x", t_))
        DMA("sp", xT[:, :, GTOK:XCOLS], xin[:, :, SEQ:SEQ + NSAMP], [], [("x", c, 2) for c in range(8)], "ld_xs")
    for g, tiles in enumerate(GROUPS if _DEBUG_STOP is None else []):
        seq = [(l, ti) + tuple(t) for l in range(DEPTH) for ti, t in enumerate(tiles)]

        def mk_next(idx):
            if idx + 1 >= len(seq):
                if g == 0:
                    c0N, nN, kindN = GROUPS[1][0]
                    return lambda: stage_h(0, 0, c0N, nN, kindN)
                return None
            lN, tiN, c0N, nN, kindN = seq[idx + 1]
            cross = lN != seq[idx][0]

            def f():
                if cross:
                    while ada_gen[0] is not None:
                        ada_tick()
                stage_h(lN, tiN, c0N, nN, kindN)
            return f
        for idx, (l, ti, c0, n, kind) in enumerate(seq):
            if ti == 0 and g == 0 and l + 1 < DEPTH:
                if l == 0:
                    def _chain():
                        yield from ada_steps(0)
                        yield from ada_steps(1)
                    ada_gen[0] = _chain()
                else:
                    ada_gen[0] = ada_steps(l + 1)
            last = ti == len(tiles) - 1
            fn = conv_tile if l % 2 == 0 else hgrn_tile
            fn(g, l, ti, c0, n, kind, last, pre_h=(idx > 0 or g > 0), next_h=mk_next(idx))
            if last:
                while ada_gen[0] is not None:
                    ada_tick()
                if g == 1 and l % 2 == 1:
                    jl = l // 2
                    DMA("sp", hgP[jl].rearrange("h k v -> k h v"), Sst[:, jl, :, :],
                        [("Sst", jl, x) for x in range(8)], [], ("st_hgP", jl), store=True)
            if l == DEPTH - 1:
                final_tile(g, ti, c0, n, kind)
    if _DEBUG_STOP is None:
      DMA("sp", convP, ustate[:], ["ustate%d_%d" % (a, b) for a in range(2) for b in range(8)], [], "st_cp", store=True)

    if _DEBUG_STOP is not None:
        g, l, ti, stage = _DEBUG_STOP
        c0, n, kind = GROUPS[g][ti]
        DMA("sp", xT[:, :, 0:GTOK], xin[:, :, g * GTOK:(g + 1) * GTOK], [],
            [("x", c, t) for c in range(8) for t in range(2)], "ld_x")
        if stage == "h":
            stage_h(l, ti, c0, n, kind)
        elif l % 2 == 0:
            conv_tile(g, l, ti, c0, n, kind, False)
        else:
            hgrn_tile(g, l, ti, c0, n, kind, False)
    S.emit(nc, es)
    es.close()
    return nc


_NC = [None]
_DEBUG_STOP = None


def _consts():
    cf = np.zeros((128, NCF), np.float32)
    t = np.arange(512)
    cf[:, C_SM32:C_SM32 + 512] = (t % 32 == 0).astype(np.float32)[None, :]
    cf[:, C_SM4:C_SM4 + 64] = (np.arange(64) % 4 == 0).astype(np.float32)[None, :]
    s = np.arange(128)[:, None]
    tt = np.arange(128)[None, :]
    cf[:, C_AM32:C_AM32 + 128] = ((s <= tt) & (s // 32 == tt // 32)).astype(np.float32)
    cf[:, C_AM2:C_AM2 + 128] = (((s // 32) % 2 == 0) & (tt // 32 == s // 32 + 1)).astype(np.float32)
    s4 = np.arange(64)[:, None]
    t4 = np.arange(64)[None, :]
    cf[:64, C_AM4:C_AM4 + 64] = ((s4 <= t4) & (s4 // 4 == t4 // 4)).astype(np.float32)
    cf[:64, C_RM:C_RM + 16] = (np.arange(64)[:, None] // 4 == np.arange(16)[None, :]).astype(np.float32)
    cf[:, C_ID:C_ID + 128] = np.eye(128, dtype=np.float32)
    return cf


def _fm(v):
    a = np.asarray(v, np.float32)
    a = a.reshape(a.shape[:-1] + (8, 128))
    return np.moveaxis(a, -1, 0)


def kernel(x_prompt, x_sample, state_conv, state_hgrn, c_prompt, c_sample, norm_g, w_ada, b_ada,
           conv_w_in, conv_w, conv_w_out, hgrn_w_in, hgrn_lower_bounds, hgrn_onorm_g, hgrn_w_out,
           final_norm_g):
    f = lambda a: np.ascontiguousarray(np.asarray(a, dtype=np.float32))
    x_prompt, x_sample, state_conv, state_hgrn = f(x_prompt), f(x_sample), f(state_conv), f(state_hgrn)
    c_prompt, c_sample = f(c_prompt), f(c_sample)
    cwi = f(conv_w_in).reshape(2, D, 4, 8, 128).transpose(0, 1, 3, 2, 4).reshape(2, D, 4 * D)
    hwi = f(hgrn_w_in).reshape(2, D, 4, 8, 128).transpose(0, 1, 3, 2, 4).reshape(2, D, 4 * D)
    w_in = np.ascontiguousarray(np.stack([cwi[0], hwi[0], cwi[1], hwi[1]]))
    cwo, hwo = f(conv_w_out), f(hgrn_w_out)
    w_out = np.ascontiguousarray(np.stack([cwo[0], hwo[0], cwo[1], hwo[1]]))
    w_ada_ = np.ascontiguousarray(f(w_ada).reshape(DEPTH, 8, 128, 24, 128).transpose(0, 3, 2, 1, 4)).reshape(
        DEPTH, 24, 128, 1024)
    vecs = np.zeros((128, NV), np.float32)
    vecs[:, OFF_NG:OFF_NG + 32] = _fm(norm_g).reshape(128, 32)
    vecs[:, OFF_FG:OFF_FG + 8] = _fm(final_norm_g).reshape(128, 8)
    vecs[:, OFF_BA:OFF_BA + 96] = np.moveaxis(f(b_ada).reshape(4, 24, 128), -1, 0).reshape(128, 96)
    vecs[:, OFF_CW:OFF_CW + 48] = _fm(conv_w).reshape(128, 48)
    vecs[:, OFF_LB:OFF_LB + 32] = _fm(hgrn_lower_bounds).reshape(128, 32)
    vecs[:, OFF_OG:OFF_OG + 16] = _fm(hgrn_onorm_g).reshape(128, 16)
    cf = _consts()
    in_maps = []
    for i in range(NCORES):
        xs = x_sample[NSEQ_S * i:NSEQ_S * (i + 1)].reshape(NSAMP, D)
        xa = np.concatenate([x_prompt[i], xs], axis=0)
        xin = np.ascontiguousarray(xa.reshape(SEQ + NSAMP, 8, 128).transpose(2, 1, 0))
        ca = np.concatenate([c_prompt[i:i + 1], c_sample[NSEQ_S * i:NSEQ_S * (i + 1)]], axis=0)
        cT = np.ascontiguousarray(ca.reshape(17, 8, 128).transpose(2, 1, 0)).reshape(128, 8 * 17)
        sc = state_conv[:, NSEQ_S * i:NSEQ_S * (i + 1)]
        scv = np.ascontiguousarray(sc.reshape(2, NSEQ_S, 2, 8, 128).transpose(4, 0, 3, 1, 2)).reshape(128, 512)
        shg = np.ascontiguousarray(state_hgrn[:, NSEQ_S * i:NSEQ_S * (i + 1)])
        in_maps.append({"xin": xin, "cT": cT, "vecs": vecs, "consts": cf, "scv": scv, "shg": shg,
                        "w_ada": w_ada_, "w_in": w_in, "w_out": w_out})
    if _NC[0] is None:
        _NC[0] = build_nc()
    res = run_bass_kernel_spmd(_NC[0], in_maps, core_ids=list(range(NCORES)))
    y_prompt = np.zeros((8, SEQ, D), np.float32)
    y_sample = np.zeros((128, TS_, D), np.float32)
    conv_p = np.zeros((2, 8, 2, D), np.float32)
    conv_s = np.zeros((2, 128, 2, D), np.float32)
    hg_p = np.zeros((2, 8, 8, 128, 128), np.float32)
    hg_s = np.zeros((2, 128, 8, 128, 128), np.float32)
    for i in range(NCORES):
        r = res.results[i]
        yo = np.asarray(r["yout"]).reshape(128, 8, SEQ + NSAMP).transpose(2, 1, 0).reshape(SEQ + NSAMP, D)
        y_prompt[i] = yo[:SEQ]
        y_sample[NSEQ_S * i:NSEQ_S * (i + 1)] = yo[SEQ:].reshape(NSEQ_S, TS_, D)
        cp = np.asarray(r["convP"]).reshape(128, 2, 8, 2)
        conv_p[:, i] = cp.transpose(1, 3, 2, 0).reshape(2, 2, D)
        cs = np.asarray(r["convS"]).reshape(128, 2, 8, NSEQ_S, 2)
        conv_s[:, NSEQ_S * i:NSEQ_S * (i + 1)] = cs.transpose(1, 3, 4, 2, 0).reshape(2, NSEQ_S, 2, D)
        hg_p[:, i] = np.asarray(r["hgP"]).reshape(2, 8, 128, 128)
        hg_s[:, NSEQ_S * i:NSEQ_S * (i + 1)] = np.asarray(r["hgS"]).reshape(2, NSEQ_S, 8, 128, 128)
    return (y_prompt, y_sample, conv_p, hg_p, conv_s, hg_s)
```

```python
import numpy as np
from contextlib import ExitStack
import concourse.bass as bass
import concourse.mybir as mybir
from concourse.bass_utils import run_bass_kernel_spmd

F32 = mybir.dt.float32
BF16 = mybir.dt.bfloat16
AF = mybir.ActivationFunctionType
ALU = mybir.AluOpType

NCORES = 8
D = 1024
DEPTH = 4
SEQ = 2048
NSEQ_S = 16
TS_ = 4
NSAMP = NSEQ_S * TS_
GTOK = 1024
XCOLS = GTOK + NSAMP
EPS = 1e-6
OFF_NG, OFF_FG, OFF_BA, OFF_CW, OFF_LB, OFF_OG, NV = 0, 32, 40, 136, 184, 216, 232
C_SM32, C_SM4, C_AM32, C_AM2, C_AM4, C_RM, C_ID, NCF = 0, 512, 576, 704, 832, 896, 912, 1040
FW = 520

GROUPS = [
    [(0, 512, "p"), (512, 512, "p")],
    [(0, 512, "p"), (512, 512, "p"), (GTOK, NSAMP, "s")],
]


class _Op:
    __slots__ = ("eng", "fn", "deps", "signal", "sigval", "sem", "isdma", "pos", "fdeps")


class Sched:
    ENGS = ("pe", "act", "dve", "pool", "sp")

    def __init__(self):
        self.q = {e: [] for e in self.ENGS}
        self.lastw = {}
        self.rd_c = {}
        self.rd_d = {}
        self.dmacount = {}
        self.stores = []

    def _mk(self, eng, fn, isdma):
        o = _Op()
        o.eng, o.fn, o.deps, o.signal, o.sigval, o.sem, o.isdma = eng, fn, set(), False, 0, None, isdma
        o.pos = len(self.q[eng])
        return o

    def _track(self, o, r, w):
        d = o.deps
        for k in r:
            x = self.lastw.get(k)
            if x is not None:
                d.add(x)
        for k in w:
            x = self.lastw.get(k)
            if x is not None:
                d.add(x)
            for y in self.rd_c.get(k, {}).values():
                d.add(y)
            for y in self.rd_d.get(k, ()):
                d.add(y)
        for k in w:
            self.lastw[k] = o
            self.rd_c[k] = {}
            self.rd_d[k] = []
        for k in r:
            if o.isdma:
                self.rd_d.setdefault(k, []).append(o)
            else:
                self.rd_c.setdefault(k, {})[o.eng] = o
        d.discard(o)
        self.q[o.eng].append(o)

    @staticmethod
    def _norm(r, w):
        isps = lambda k: isinstance(k, tuple) and k[0] == "ps"
        w2 = [k[:2] if isps(k) else k for k in w] + [k[:2] for k in r if isps(k)]
        r2 = [k for k in r if not isps(k)]
        return r2, w2

    def op(self, eng, fn, r=(), w=()):
        o = self._mk(eng, fn, False)
        r, w = self._norm(r, w)
        self._track(o, r, w)
        return o

    def dma(self, eng, fn, r=(), w=(), sk=None, store=False):
        o = self._mk(eng, fn, True)
        c = self.dmacount.get(sk, 0) + 1
        self.dmacount[sk] = c
        o.sem = sk
        o.sigval = 16 * c
        self._track(o, r, w)
        if store:
            self.stores.append(o)
        return o

    def emit(self, nc, es):
        fin = self._mk("sp", lambda e: e.nop(), False)
        fin.deps = set(self.stores)
        self.q["sp"].append(fin)
        for e in self.ENGS:
            for o in self.q[e]:
                best = {}
                fd = []
                for dd in o.deps:
                    if dd.isdma:
                        fd.append(dd)
                        continue
                    if dd.eng == e:
                        if e == "pe" or o.isdma and False:
                            continue
                    b = best.get(dd.eng)
                    if b is None or dd.pos > b.pos:
                        best[dd.eng] = dd
                for dd in best.values():
                    dd.signal = True
                    fd.append(dd)
                o.fdeps = fd
        esem = {e: es.enter_context(nc.semaphore("c_" + e)) for e in self.ENGS}
        dsem = {}
        for i, k in enumerate(self.dmacount):
            dsem[k] = es.enter_context(nc.semaphore("d%d" % i))
        for e in self.ENGS:
            c = 0
            for o in self.q[e]:
                if o.isdma:
                    o.sem = dsem[o.sem]
                else:
                    o.sem = esem[e]
                    if o.signal:
                        c += 1
                        o.sigval = c
        block = es.enter_context(nc.Block())

        def run(ename):
            def body(eng):
                waited = {}
                for o in self.q[ename]:
                    for dd in o.fdeps:
                        if waited.get(dd.sem, 0) < dd.sigval:
                            eng.wait_ge(dd.sem, dd.sigval)
                            waited[dd.sem] = dd.sigval
                    ins = o.fn(eng)
                    if o.isdma:
                        ins.then_inc(o.sem, 16)
                    elif o.signal:
                        ins.then_inc(o.sem, 1)
            return body

        block.tensor(run("pe"))
        block.scalar(run("act"))
        block.vector(run("dve"))
        block.gpsimd(run("pool"))
        block.sync(run("sp"))


class Ring:
    def __init__(self, name, bufs):
        self.name, self.bufs, self.i = name, bufs, 0

    def next(self):
        i = self.i % len(self.bufs)
        self.i += 1
        return self.bufs[i], (self.name, i)


def build_nc():
    nc = bass.Bass("TRN2", target_bir_lowering=False)
    S = Sched()
    es = ExitStack()

    def din(name, shape):
        return nc.dram_tensor(name, shape, F32, kind="ExternalInput").ap()

    def dout(name, shape):
        return nc.dram_tensor(name, shape, F32, kind="ExternalOutput").ap()

    xin = din("xin", [128, 8, SEQ + NSAMP])
    cT_d = din("cT", [128, 8 * 17])
    vecs_d = din("vecs", [128, NV])
    cf_d = din("consts", [128, NCF])
    scv_d = din("scv", [128, 512])
    shg_d = din("shg", [2, NSEQ_S, 8, 128, 128])
    wada_d = din("w_ada", [DEPTH, 24, 128, 8 * 128])
    win_d = din("w_in", [DEPTH, D, 4 * D])
    wout_d = din("w_out", [DEPTH, D, D])
    yout = dout("yout", [128, 8, SEQ + NSAMP])
    convP = dout("convP", [128, 32])
    convS = dout("convS", [128, 512])
    hgP = dout("hgP", [2, 8, 128, 128])
    hgS = dout("hgS", [2, NSEQ_S, 8, 128, 128])

    def sb(name, shape, dt=F32):
        return es.enter_context(nc.sbuf_tensor("s_" + name, shape, dt))

    xT = sb("xT", [128, 8, XCOLS])
    hT = sb("hT", [128, 8, 512], BF16)
    sqy = sb("sqy", [128, 8, 512], BF16)
    NWIN, NWOUT, NWADA = 8, 8, 2
    win_s = [sb("win%d" % i, [128, 8, 512], BF16) for i in range(NWIN)]
    wout_s = [sb("wout%d" % i, [128, 1024], BF16) for i in range(NWOUT)]
    wada_s = [sb("wada%d" % i, [128, 8, 128], BF16) for i in range(NWADA)]
    fr = Ring("f", [sb("fr%d" % i, [128, FW]) for i in range(5)])
    br = Ring("b", [sb("br%d" % i, [128, 512], BF16) for i in range(7)])
    rstd_t = sb("rstd_t", [128, 512])
    qt = sb("qt", [128, 4, 512], BF16)
    kt = sb("kt", [128, 4, 512], BF16)
    qx = sb("qx", [128, 4, 512], BF16)
    D2 = sb("D2", [128, 8, 16])
    dq = sb("dq", [128, 16])
    khtm = sb("khtm", [128, 4, 4, 128], BF16)
    vtm = sb("vtm", [128, 4, 4, 128], BF16)
    szb = sb("szb", [128, 4, 512], BF16)
    osq = sb("osq", [128, 4, 128], BF16)
    Sst = sb("Sst", [128, 2, 8, 128])
    Sbf = sb("Sbf", [128, 4, 128], BF16)
    stg = [sb("stg%d" % i, [128, 4, 128]) for i in range(2)]
    ustate = sb("ustate", [128, 32])
    scvt = [sb("scvt%d" % i, [128, 32]) for i in range(2)]
    usmt = [sb("usmt%d" % i, [128, 32]) for i in range(2)]
    S2 = sb("S2", [128, 4, 128])
    cf = sb("cf", [128, C_ID])
    vecs = sb("vecs", [128, NV])
    mod = sb("mod", [128, DEPTH, 24, 17])
    cT = sb("cT", [128, 8 * 17])
    scT = sb("scT", [128, 8, 17], BF16)
    ident = sb("ident", [128, 128], BF16)
    ones = sb("ones", [128, 128], BF16)
    dch = sb("dch", [128, 8, 16])
    lbw = sb("lbw", [128, 128])
    psb = [es.enter_context(nc.psum_tensor("ps%d" % i, [128, 512], F32)) for i in range(8)]

    class PRing:
        def __init__(self, banks):
            self.banks, self.i = banks, 0

        def next(self):
            b = self.banks[self.i % len(self.banks)]
            self.i += 1
            return psb[b], ("ps", b)

    proj = PRing([0, 1, 2])
    tpr = PRing([7])

    class SubRing:
        def __init__(self, bank):
            self.bank, self.i = bank, 0

        def next(self):
            s = self.i % 4
            self.i += 1
            return psb[self.bank][:, s * 128:(s + 1) * 128], ("ps", self.bank, s)

    aring = SubRing(5)
    kvring = SubRing(6)

    def MM(out, lhsT, rhs, st, sp, r, w, **kw):
        S.op("pe", lambda e: e.matmul(out, lhsT, rhs, start=st, stop=sp, **kw), r, w)

    def ACT(out, in_, func, r, w, bias=0.0, scale=1.0):
        S.op("act", lambda e: e.activation(out, in_, func, bias=bias, scale=scale), r, w)

    def TT(eng, out, a, b, op, r, w):
        S.op(eng, lambda e: e.tensor_tensor(out, a, b, op), r, w)

    def TSC(eng, out, a, s1, s2, op0, op1, r, w):
        S.op(eng, lambda e: e.tensor_scalar(out, a, s1, s2, op0, op1), r, w)

    def STT(out, in0, sc, in1, op0, op1, r, w):
        S.op("dve", lambda e: e.scalar_tensor_tensor(out, in0, sc, in1, op0, op1), r, w)

    def CP(eng, out, in_, r, w):
        S.op(eng, lambda e: e.tensor_copy(out, in_), r, w)

    def DMA(q, out, in_, r, w, sk, store=False):
        S.dma(q, lambda e: e.dma_start(out=out, in_=in_), r, w, sk, store)

    class Stream:
        def __init__(self, name, slots, total, src, extra=None):
            self.name, self.slots, self.total, self.src, self.nl = name, slots, total, src, 0
            self.extra = extra or (lambda s_: [])

        def load_next(self):
            if self.nl >= self.total:
                return
            i = self.nl
            self.nl += 1
            s = i % len(self.slots)
            DMA("pool", self.slots[s][:], self.src(i), [], [(self.name, s)] + self.extra(s), (self.name, s))

        def slot(self, i):
            s = i % len(self.slots)
            return self.slots[s], (self.name, s)

    def win_src(i):
        l, j = (i // 8) % DEPTH, i % 8
        return win_d[l].rearrange("(k p) e -> p k e", p=128)[:, :, j * 512:(j + 1) * 512]

    def wout_src(i):
        l, j = (i // 8) % DEPTH, i % 8
        return wout_d[l][j * 128:(j + 1) * 128, :]

    def wada_src(i):
        l, q = i // 24, i % 24
        if l == 0:
            q = (list(range(8, 16)) + list(range(0, 8)) + list(range(16, 24)))[q]
        return wada_d[l, q].rearrange("p (k e) -> p k e", e=128)

    winS = Stream("win", win_s, 2 * DEPTH * 8, win_src)
    woutS = Stream("wout", wout_s, 2 * DEPTH * 8, wout_src)
    wadaS = Stream("wada", wada_s, DEPTH * 24, wada_src)
    a0_slots, a0_keys = [], []
    for t_, nm in ((qt, "qt"), (kt, "kt"), (szb, "szb")):
        for hlf in range(2):
            a0_slots.append(t_[:, 2 * hlf:2 * hlf + 2, :].rearrange("p a (b c) -> p (a b) c", c=128))
            a0_keys.append([(nm, 2 * hlf), (nm, 2 * hlf + 1)])
    for hlf in range(2):
        a0_slots.append(khtm[:, 2 * hlf:2 * hlf + 2, :, :].rearrange("p a b c -> p (a b) c"))
        a0_keys.append([("khtm", x) for x in range(4)])
        a0_slots.append(vtm[:, 2 * hlf:2 * hlf + 2, :, :].rearrange("p a b c -> p (a b) c"))
        a0_keys.append([("vtm", 2 * hlf), ("vtm", 2 * hlf + 1)])
    wada0S = Stream("wada0", a0_slots, 24, wada_src, extra=lambda s_: a0_keys[s_])
    wadaS.nl = 24

    DMA("sp", cT[:], cT_d, [], ["cT"], "ld_cT")
    DMA("sp", vecs[:], vecs_d, [], ["vecs"], "ld_vecs")
    DMA("sp", cf[:], cf_d[:, 0:C_ID], [], ["cf"], "ld_cf")
    DMA("pool", ident[:], cf_d[:, C_ID:C_ID + 128], [], ["ident"], "ld_id")
    for _ in range(len(a0_slots)):
        wada0S.load_next()
    S.op("pool", lambda e: e.memset(ones[:], 1.0), [], ["ones"])
    S.op("pool", lambda e: e.memset(dq[:], 1.0), [], ["dq"])
    S.op("pool", lambda e: e.memset(ustate[:], 0.0), [], ["ustate%d_%d" % (a, b) for a in range(2) for b in range(8)])
    S.op("pool", lambda e: e.memset(Sst[:], 0.0), [], [("Sst", a, b) for a in range(2) for b in range(8)])
    ACT(scT[:].rearrange("p a b -> p (a b)"), cT[:], AF.Silu, ["cT"], ["scT"])
    lbin = vecs[:, OFF_LB:OFF_LB + 32]
    mx, ex, sm = lbw[:, 0:8], lbw[:, 8:40], lbw[:, 40:48]
    TT("dve", mx, lbin[:, 0:8], lbin[:, 8:16], ALU.max, ["vecs"], ["lb_mx"])
    TT("dve", mx, mx, lbin[:, 16:24], ALU.max, ["vecs", "lb_mx"], ["lb_mx"])
    TT("dve", mx, mx, lbin[:, 24:32], ALU.max, ["vecs", "lb_mx"], ["lb_mx"])
    TT("dve", ex.rearrange("p (l h) -> p l h", h=8), lbin.rearrange("p (l h) -> p l h", h=8),
       mx.unsqueeze(1).to_broadcast([128, 4, 8]), ALU.subtract, ["vecs", "lb_mx"], ["lb_ex"])
    ACT(ex, ex, AF.Exp, ["lb_ex"], ["lb_ex"])
    TT("dve", sm, ex[:, 0:8], ex[:, 8:16], ALU.add, ["lb_ex"], ["lb_sm"])
    TT("dve", sm, sm, ex[:, 16:24], ALU.add, ["lb_ex", "lb_sm"], ["lb_sm"])
    TT("dve", sm, sm, ex[:, 24:32], ALU.add, ["lb_ex", "lb_sm"], ["lb_sm"])
    S.op("dve", lambda e: e.reciprocal(sm, sm), ["lb_sm"], ["lb_sm"])
    TT("dve", ex.rearrange("p (l h) -> p l h", h=8), ex.rearrange("p (l h) -> p l h", h=8),
       sm.unsqueeze(1).to_broadcast([128, 4, 8]), ALU.mult, ["lb_ex", "lb_sm"], ["lb_ex"])
    LB, OML, NOML = 48, 64, 80
    CP("dve", lbw[:, LB:LB + 8], ex[:, 8:16], ["lb_ex"], ["lb0"])
    TT("dve", lbw[:, LB + 8:LB + 16], ex[:, 8:16], ex[:, 16:24], ALU.add, ["lb_ex"], ["lb1"])
    TT("dve", lbw[:, LB + 8:LB + 16], lbw[:, LB + 8:LB + 16], ex[:, 24:32], ALU.add, ["lb_ex", "lb1"], ["lb1"])
    TSC("dve", lbw[:, OML:OML + 16], lbw[:, LB:LB + 16], -1.0, 1.0, ALU.mult, ALU.add, ["lb0", "lb1"], ["oml"])
    TSC("dve", lbw[:, NOML:NOML + 16], lbw[:, LB:LB + 16], 1.0, -1.0, ALU.mult, ALU.add, ["lb0", "lb1"], ["noml"])
    LBH, HOML, NHOML = 96, 64, 80
    S.op("dve", lambda e: e.scalar_tensor_tensor(lbw[:, LBH:LBH + 16], lbw[:, OML:OML + 16], 0.5, lbw[:, LB:LB + 16],
                                                  ALU.mult, ALU.add), ["oml", "lb0", "lb1"], ["lbh"])
    TSC("dve", lbw[:, HOML:HOML + 16], lbw[:, OML:OML + 16], 0.5, None, ALU.mult, ALU.bypass, ["oml", "lbh"], ["oml"])
    TSC("dve", lbw[:, NHOML:NHOML + 16], lbw[:, NOML:NOML + 16], 0.5, None, ALU.mult, ALU.bypass, ["noml"], ["noml"])
    S.op("dve", lambda e: e.engine_nop(), ["oml", "noml", "lbh"], ["lbc"])

    ORDER0 = list(range(8, 16)) + list(range(0, 8)) + list(range(16, 24))

    def ada_part(l, positions, fin):
        strm = wada0S if l == 0 else wadaS
        for p_ in positions:
            q = ORDER0[p_] if l == 0 else p_
            i = l * 24 + p_
            slot, sk = strm.slot(i)
            bank, bk = proj.next()
            for k in range(8):
                MM(bank[:, 0:17], slot[:, k, :], scT[:, k, :], k == 0, k == 7,
                   [sk, "scT"] + strm.extra(i % len(strm.slots)), [bk])
            strm.load_next()
            TSC("dve", mod[:, l, q, :], bank[:, 0:17], vecs[:, OFF_BA + 24 * l + q:OFF_BA + 24 * l + q + 1], None,
                ALU.add, ALU.bypass, [bk, "vecs"], [("mod", l, q // 8)])
            yield
        if fin:
            TSC("dve", mod[:, l, 8:16, :], mod[:, l, 8:16, :], 1.0, None, ALU.add, ALU.bypass,
                [("mod", l, 1)], [("mod", l, 1)])
            TT("dve", mod[:, l, 8:16, :], mod[:, l, 8:16, :],
               vecs[:, OFF_NG + 8 * l:OFF_NG + 8 * (l + 1)].unsqueeze(2).to_broadcast([128, 8, 17]),
               ALU.mult, [("mod", l, 1), "vecs"], [("mod", l, 1)])
            yield

    def ada_steps(l):
        if l == 0:
            return iter(())
        return ada_part(l, range(24), True)

    def _ada0():
        yield from ada_part(0, range(0, 16), True)
        yield from ada_part(0, range(16, 24), False)

    for i_, _ in enumerate(_ada0()):
        if i_ % 3 == 0 and i_ // 3 < NWIN:
            winS.load_next()
    while winS.nl < NWIN:
        winS.load_next()
    for _ in range(NWOUT):
        woutS.load_next()
    for _ in range(NWADA):
        wadaS.load_next()
    ada_gen = [None]

    def ada_tick():
        if ada_gen[0] is not None:
            try:
                next(ada_gen[0])
            except StopIteration:
                ada_gen[0] = None

    def xkeys(ti):
        return [("x", c, ti) for c in range(8)]

    def stage_norm(l, ti, c0, n, kind, g_vec_off=None, final=False):
        bank, bk = proj.next()
        for c in range(8):
            sq, sqk = br.next()
            ACT(sq[:, 0:n], xT[:, c, c0:c0 + n], AF.Square, [("x", c, ti)], [sqk])
            MM(bank[:, 0:n], ones[:, :], sq[:, 0:n], c == 0, c == 7, ["ones", sqk], [bk])
        rstd, rk = rstd_t, "rstd"
        ACT(rstd[:, 0:n], bank[:, 0:n], AF.Ln, [bk], [rk], bias=EPS, scale=1.0 / D)
        ACT(rstd[:, 0:n], rstd[:, 0:n], AF.Exp, [rk], [rk], scale=-0.5)
        return rstd, rk

    def stage_h(l, ti, c0, n, kind):
        rstd, rk = stage_norm(l, ti, c0, n, kind)
        if kind == "p":
            for c in range(8):
                tmp, tk = fr.next()
                STT(tmp[:, 0:n], xT[:, c, c0:c0 + n], mod[:, l, 8 + c, 0:1], rstd[:, 0:n], ALU.mult, ALU.mult,
                    [("x", c, ti), ("mod", l, 1), rk], [tk])
                ACT(hT[:, c, 0:n], tmp[:, 0:n], AF.Identity, [tk, ("mod", l, 0)], [("h", c)],
                    bias=mod[:, l, c, 0:1], scale=1.0)
        else:
            tmp, tk = fr.next()
            tv = tmp[:, 0:512].rearrange("p (c t) -> p c t", c=8)
            TT("dve", tv, xT[:, :, c0:c0 + n], rstd[:, 0:n].unsqueeze(1).to_broadcast([128, 8, n]), ALU.mult,
               xkeys(ti) + [rk], [tk])
            tv4 = tmp[:, 0:512].rearrange("p (c s t) -> p c s t", c=8, t=TS_)
            TT("dve", tv4, tv4, mod[:, l, 8:16, 1:17].unsqueeze(3).to_broadcast([128, 8, NSEQ_S, TS_]), ALU.mult,
               [tk, ("mod", l, 1)], [tk])
            TT("dve", hT[:, :, 0:n].rearrange("p c (s t) -> p c s t", t=TS_), tv4,
               mod[:, l, 0:8, 1:17].unsqueeze(3).to_broadcast([128, 8, NSEQ_S, TS_]), ALU.add,
               [tk, ("mod", l, 0)], [("h", c) for c in range(8)])

    def stage_out(g, l, ti, c0, n, kind):
        base = (g * DEPTH + l) * 8
        for m in range(8):
            bank, bk = proj.next()
            for j in range(8):
                slot, sk = woutS.slot(base + j)
                MM(bank[:, 0:n], slot[:, m * 128:(m + 1) * 128], sqy[:, j, 0:n], j == 0, j == 7,
                   [sk, ("sqy", j)], [bk])
            if kind == "p":
                STT(xT[:, m, c0:c0 + n], bank[:, 0:n], mod[:, l, 16 + m, 0:1], xT[:, m, c0:c0 + n],
                    ALU.mult, ALU.add, [bk, ("mod", l, 2), ("x", m, ti)], [("x", m, ti)])
            else:
                tmp, tk = fr.next()
                t3 = tmp[:, 0:n].rearrange("p (s t) -> p s t", t=TS_)
                TT("dve", t3, bank[:, 0:n].rearrange("p (s t) -> p s t", t=TS_),
                   mod[:, l, 16 + m, 1:17].unsqueeze(2).to_broadcast([128, NSEQ_S, TS_]), ALU.mult,
                   [bk, ("mod", l, 2)], [tk])
                TT("dve", xT[:, m, c0:c0 + n], xT[:, m, c0:c0 + n], tmp[:, 0:n], ALU.add,
                   [tk, ("x", m, ti)], [("x", m, ti)])

    def conv_tile(g, l, ti, c0, n, kind, last, pre_h=False, next_h=None):
        jl = l // 2
        base = (g * DEPTH + l) * 8
        proj.banks = [0, 1, 2, 3, 4, 5, 6, 7]
        if not pre_h:
            stage_h(l, ti, c0, n, kind)
        hk = [("h", c) for c in range(8)]
        for j in range(8):
            slot, sk = winS.slot(base + j)
            def grp(off):
                bank, bk = proj.next()
                for k in range(8):
                    MM(bank[:, 0:n], slot[:, k, off:off + 128], hT[:, k, 0:n], k == 0, k == 7, [sk] + hk, [bk])
                return bank, bk
            pv, pvk = grp(256)
            vs, vk = fr.next()
            ACT(vs[:, 0:n], pv[:, 0:n], AF.Copy, [pvk], [vk])
            pc, pck = grp(128)
            ue, uk = fr.next()
            cw = [vecs[:, OFF_CW + (jl * 3 + t) * 8 + j:OFF_CW + (jl * 3 + t) * 8 + j + 1] for t in range(3)]
            t1, t1k = fr.next()
            if kind == "p":
                usl = ustate[:, (jl * 8 + j) * 2:(jl * 8 + j) * 2 + 2]
                CP("pool", ue[:, 0:2], usl, ["ustate%d_%d" % (jl, j)], [uk])
                TT("dve", ue[:, 2:n + 2], pc[:, 0:n], vs[:, 0:n], ALU.mult, [pck, vk, uk], [uk])
                CP("pool", usl, ue[:, n:n + 2], [uk], ["ustate%d_%d" % (jl, j)])
                u0, u1, u2 = ue[:, 0:n], ue[:, 1:n + 1], ue[:, 2:n + 2]
                t1v = t1[:, 0:n]
            else:
                ue3 = ue[:, 0:NSEQ_S * 6].rearrange("p (s t) -> p s t", t=6)
                cs = (jl * 8 + j) * 32
                si = j % 2
                DMA("sp", scvt[si][:], scv_d[:, cs:cs + 32], [], [("scvt", si)], ("scvt", si))
                CP("pool", ue3[:, :, 0:2], scvt[si][:].rearrange("p (s t) -> p s t", t=2), [("scvt", si)], [uk])
                TT("dve", ue3[:, :, 2:6], pc[:, 0:n].rearrange("p (s t) -> p s t", t=TS_),
                   vs[:, 0:n].rearrange("p (s t) -> p s t", t=TS_), ALU.mult, [pck, vk, uk], [uk])
                CP("pool", usmt[si][:].rearrange("p (s t) -> p s t", t=2), ue3[:, :, 4:6], [uk], [("usmt", si)])
                DMA("sp", convS[:, cs:cs + 32], usmt[si][:], [("usmt", si)], [], ("usmt_st", si), store=True)
                u0, u1, u2 = ue3[:, :, 0:4], ue3[:, :, 1:5], ue3[:, :, 2:6]
                t1v = t1[:, 0:n].rearrange("p (s t) -> p s t", t=TS_)
            ACT(t1v, u0, AF.Identity, [uk, "vecs"], [t1k], scale=cw[0])
            STT(t1v, u1, cw[1], t1v, ALU.mult, ALU.add, [uk, t1k, "vecs"], [t1k])
            STT(t1v, u2, cw[2], t1v, ALU.mult, ALU.add, [uk, t1k, "vecs"], [t1k])
            ada_tick()
            pz, pzk = grp(384)
            sz, szk = fr.next()
            ACT(sz[:, 0:n], pz[:, 0:n], AF.Silu, [pzk], [szk])
            pbb, pbk = grp(0)
            TT("dve", t1[:, 0:n], pbb[:, 0:n], t1[:, 0:n], ALU.mult, [pbk, t1k], [t1k])
            TT("dve", sqy[:, j, 0:n], t1[:, 0:n], sz[:, 0:n], ALU.mult, [t1k, szk], [("sqy", j)])
            if last:
                winS.load_next()
            ada_tick()
        if next_h is not None:
            next_h()
        stage_out(g, l, ti, c0, n, kind)
        if last:
            for _ in range(8):
                woutS.load_next()

    def hgrn_tile(g, l, ti, c0, n, kind, last, pre_h=False, next_h=None):
        jl = l // 2
        base = (g * DEPTH + l) * 8
        proj.banks = [0, 1, 2]
        if not pre_h:
            stage_h(l, ti, c0, n, kind)
        hk = [("h", c) for c in range(8)]
        if kind == "p":
            L, nb, nblk, cpb = 32, 128, n // 128, 4
            smask = cf[:, C_SM32:C_SM32 + n]
            amask = cf[:, C_AM32:C_AM32 + 128]
        else:
            L, nb, nblk, cpb = TS_, NSAMP, 1, NSEQ_S
            smask = cf[:, C_SM4:C_SM4 + n]
            amask = cf[:, C_AM4:C_AM4 + NSAMP]
        nch = n // L
        Skh = lambda h: ("Sst", jl, h)
        for hf in range(2):
            hs = list(range(4 * hf, 4 * hf + 4))
            proj.banks = [0, 1, 2, 4, 6, 7, 5]

            def grp(slot, sk, off):
                bank, bk = proj.next()
                for k in range(8):
                    MM(bank[:, 0:n], slot[:, k, off:off + 128], hT[:, k, 0:n], k == 0, k == 7, [sk] + hk, [bk])
                return bank, bk
            def P1h(hh):
                h = hs[hh]
                slot, sk = winS.slot(base + h)
                bq, bqk = grp(slot, sk, 0)
                ACT(qt[:, hh, 0:n], bq[:, 0:n], AF.Silu, [bqk], [("qt", hh)])
                bf_, bfk = grp(slot, sk, 128)
                th, thk = fr.bufs[hh], ("f", hh)
                ACT(th[:, 0:n], bf_[:, 0:n], AF.Tanh, [bfk], [thk], scale=0.5)
                bz, bzk = grp(slot, sk, 384)
                ACT(szb[:, hh, 0:n], bz[:, 0:n], AF.Silu, [bzk], [("szb", hh)])
                ogc = OFF_OG + jl * 8 + h
                TSC("pool", szb[:, hh, 0:n], szb[:, hh, 0:n], vecs[:, ogc:ogc + 1], 1.0, ALU.mult, ALU.mult,
                    [("szb", hh), "vecs"], [("szb", hh)])
                ada_tick()
            pend = []

            def flush_tr():
                for (khT, khk, hh) in pend:
                    tb, tbk = proj.next()
                    tbb = tb[:, :].bitcast(BF16)
                    for blk in range(nblk):
                        S.op("pe", lambda e, o=tbb[0:nb, blk * 128:(blk + 1) * 128], i=khT[:, blk * nb:(blk + 1) * nb]:
                             e.transpose(o, i, ident[:, :]), [khk, "ident"], [tbk])
                    ACT(khtm[0:nb, 0:nblk, hh, :], tbb[0:nb, 0:nblk * 128].rearrange("p (b k) -> p b k", k=128),
                        AF.Copy, [tbk], [("khtm", hh)])
                del pend[:]
            pend_v = []

            def flush_v():
                for (bv, bvk, blk) in pend_v:
                    ACT(vtm[0:nb, blk, :, :].rearrange("p h v -> p (h v)"), bv[0:nb, 0:512], AF.Copy, [bvk],
                        [("vtm", blk)])
                del pend_v[:]

            def vproj(blk):
                bv, bvk = proj.next()
                for hh2, h2 in enumerate(hs):
                    slot, sk = winS.slot(base + h2)
                    for k in range(8):
                        MM(bv[0:nb, hh2 * 128:(hh2 + 1) * 128], hT[:, k, blk * nb:(blk + 1) * nb], slot[:, k, 256:384],
                           k == 0, k == 7, [sk] + hk, [bvk])
                pend_v.append((bv, bvk, blk))
            hst = {}
            G, Gk = fr.bufs[4], ("f", 4)

            def stA(hh):
                h = hs[hh]
                col = jl * 8 + h
                th, thk = fr.bufs[hh], ("f", hh)
                kk, kkk = br.next()
                TSC("dve", kk[:, 0:n], th[:, 0:n], lbw[:, NHOML + col:NHOML + col + 1],
                    lbw[:, HOML + col:HOML + col + 1], ALU.mult, ALU.add, [thk, "lbc"], [kkk])
                TSC("dve", th[:, 0:n], th[:, 0:n], lbw[:, HOML + col:HOML + col + 1],
                    lbw[:, LBH + col:LBH + col + 1], ALU.mult, ALU.add, [thk, "lbc"], [thk])
                hst[hh] = {"kk": (kk, kkk)}

            def stB(hh):
                th, thk = fr.bufs[hh], ("f", hh)
                S.op("dve", lambda e, G=G, lf=th, smask=smask: e.tensor_tensor_scan(
                    G[:, 0:n], smask, lf[:, 0:n], 0.0, ALU.max, ALU.mult), [thk, "cf"], [Gk])

            def stC(hh):
                h = hs[hh]
                eG, eGk = br.next()
                ACT(eG[:, 0:n], G[:, 0:n], AF.Copy, [Gk], [eGk])
                enG, enGk = br.next()
                def _rc(e, o=enG[:, 0:n], i=G[:, 0:n]):
                    with nc.allow_low_precision("1/decay feeds a bf16 matmul operand"):
                        return e.reciprocal(o, i)
                S.op("dve", _rc, [Gk], [enGk])
                ACT(dch[:, h, 0:nch], G[:, L - 1:n:L], AF.Copy, [Gk], [("dch", h)])
                hst[hh]["eG"] = (eG, eGk)
                hst[hh]["enG"] = (enG, enGk)

            def stD(hh):
                kk, kkk = hst[hh]["kk"]
                eG, eGk = hst[hh]["eG"]
                enG, enGk = hst[hh]["enG"]
                TT("dve", qt[:, hh, 0:n], qt[:, hh, 0:n], eG[:, 0:n], ALU.mult, [("qt", hh), eGk], [("qt", hh)])
                TT("dve", kt[:, hh, 0:n], kk[:, 0:n], enG[:, 0:n], ALU.mult, [kkk, enGk], [("kt", hh)])

            def stE(hh):
                h = hs[hh]
                if kind == "p":
                    CP("pool", D2[:, h, 0:nch], dch[:, h, 0:nch], [("dch", h)], [("D2", h)])
                    TT("pool", D2[:, h, 0:nch:2], D2[:, h, 0:nch:2], dch[:, h, 1:nch:2], ALU.mult,
                       [("D2", h), ("dch", h)], [("D2", h)])
                    CP("pool", dq[:, 1:nch:2], dch[:, h, 0:nch:2], [("dch", h)], ["dq"])
                    dsel, dselk = D2, ("D2", h)
                else:
                    dsel, dselk = dch, ("dch", h)
                khT, khk = br.next()
                TT("pool", khT[:, 0:n].rearrange("p (c t) -> p c t", t=L),
                   kt[:, hh, 0:n].rearrange("p (c t) -> p c t", t=L),
                   dsel[:, h, 0:nch].unsqueeze(2).to_broadcast([128, nch, L]), ALU.mult,
                   [("kt", hh), dselk], [khk])
                if kind == "p":
                    TT("pool", qx[:, hh, 0:n].rearrange("p (c t) -> p c t", t=L),
                       qt[:, hh, 0:n].rearrange("p (c t) -> p c t", t=L),
                       dq[:, 0:nch].unsqueeze(2).to_broadcast([128, nch, L]), ALU.mult,
                       [("qt", hh), "dq"], [("qx", hh)])
                pend.append((khT, khk, hh))

            P1h(0)
            stA(0)
            stB(0)
            for hh, h in enumerate(hs):
                if hh + 1 < 4:
                    P1h(hh + 1)
                stC(hh)
                stD(hh)
                flush_tr()
                stE(hh)
                if hh < nblk:
                    vproj(hh)
                if hh + 1 < 4:
                    stA(hh + 1)
                    stB(hh + 1)
                flush_v()
            flush_tr()
            flush_v()
            if last:
                for _ in range(4):
                    winS.load_next()
            proj.banks = [0, 1, 2]
            if kind == "p":
                ACT(Sbf[:, 0:4, :], Sst[:, jl, 4 * hf:4 * hf + 4, :], AF.Copy, [Skh(h) for h in hs],
                    [("Sbf", h) for h in hs])
            else:
                stgA = [stg[0][:], stg[1][:],
                        qx[:, 0:2, :].bitcast(F32).rearrange("p a (b v) -> p (a b) v", v=128),
                        qx[:, 2:4, :].bitcast(F32).rearrange("p a (b v) -> p (a b) v", v=128)]
                NSTG = 4

                def stgk(i):
                    ks = [("stg", i, x) for x in range(4)]
                    if i >= 2:
                        ks += [("qx", 2 * (i - 2)), ("qx", 2 * (i - 2) + 1)]
                    return ks

                def ld(s):
                    i = s % NSTG
                    DMA("sp", stgA[i], shg_d[jl, s, 4 * hf:4 * hf + 4].rearrange("h k v -> k h v"),
                        [], stgk(i), ("stg", i))
                for s_ in range(NSTG):
                    ld(s_)

            nst = {}

            pob = (lambda blk: 3 + (blk % 3)) if kind == "p" else (lambda blk: 3 + (blk % 2))

            nst = {}

            def norm_sq(blk):
                ob = pob(blk)
                po, pok = psb[ob], ("ps", ob)
                for hh in range(4):
                    ACT(osq[:, hh, 0:nb], po[:, hh * 128:hh * 128 + nb], AF.Square, [pok], [("osq", hh)])

            def norm_nm(blk):
                br_, brk = proj.next()
                for hh in range(4):
                    MM(br_[:, hh * 128:hh * 128 + nb], ones[:, :], osq[:, hh, 0:nb], True, True,
                       ["ones", ("osq", hh)], [brk])
                nst[blk] = (br_, brk)

            def norm_p1(blk):
                norm_sq(blk)
                norm_nm(blk)

            def norm_p2(blk):
                br_, brk = nst[blk]
                brv = br_[:, :].rearrange("p (h t) -> p h t", h=4)[:, :, 0:nb]
                lnr, lnk = fr.bufs[4], ("f", 4)
                lnrv = lnr[:, 0:512].rearrange("p (h t) -> p h t", h=4)[:, :, 0:nb]
                ACT(lnrv, brv, AF.Ln, [brk], [lnk], bias=EPS, scale=1.0 / 128)
                ACT(lnrv, lnrv, AF.Exp, [lnk], [lnk], scale=-0.5)

            def norm_p3(blk):
                ob = pob(blk)
                po, pok = psb[ob], ("ps", ob)
                bc0 = blk * nb
                lnr, lnk = fr.bufs[4], ("f", 4)
                t1, t1k = fr.bufs[0], ("f", 0)
                t1v = t1[:, 0:512].rearrange("p (h t) -> p h t", h=4)[:, :, 0:nb]
                lnrv = lnr[:, 0:512].rearrange("p (h t) -> p h t", h=4)[:, :, 0:nb]
                pov = po[:, 0:512].rearrange("p (h t) -> p h t", h=4)[:, :, 0:nb]
                TT("dve", t1v, pov, lnrv, ALU.mult, [pok, lnk], [t1k])
                TT("pool", sqy[:, 4 * hf:4 * hf + 4, bc0:bc0 + nb], t1v, szb[:, :, bc0:bc0 + nb], ALU.mult,
                   [t1k] + [("szb", x) for x in range(4)], [("sqy", h) for h in hs])

            def emit_A(blk):
                bc0 = blk * nb
                if kind == "p":
                    res = []
                    for rnd in range(2):
                        pa, pak = proj.next()
                        for hl in range(2):
                            hh = 2 * rnd + hl
                            MM(pa[:, (2 * hl) * 128:(2 * hl + 1) * 128], kt[:, hh, bc0:bc0 + 128],
                               qt[:, hh, bc0:bc0 + 128], True, True, [("kt", hh), ("qt", hh)], [pak])
                            MM(pa[:, (2 * hl + 1) * 128:(2 * hl + 2) * 128], kt[:, hh, bc0:bc0 + 128],
                               qx[:, hh, bc0:bc0 + 128], True, True, [("kt", hh), ("qx", hh)], [pak])
                        atm, atk = br.next()
                        TT("dve", atm[:, :].rearrange("p (h t) -> p h t", h=2),
                           pa[:, :].rearrange("p (h t) -> p h t", h=2),
                           cf[:, C_AM32:C_AM32 + 256].unsqueeze(1).to_broadcast([128, 2, 256]), ALU.mult,
                           [pak, "cf"], [atk])
                        res.append((atm, atk))
                    return res
                pa, pak = psb[5], ("ps", 5)
                for hh, h in enumerate(hs):
                    MM(pa[0:nb, hh * 128:hh * 128 + nb], kt[:, hh, bc0:bc0 + nb], qt[:, hh, bc0:bc0 + nb], True, True,
                       [("kt", hh), ("qt", hh)], [pak])
                atm, atk = br.next()
                TT("dve", atm[0:nb, :].rearrange("p (h t) -> p h t", h=4)[:, :, 0:nb],
                   pa[0:nb, :].rearrange("p (h t) -> p h t", h=4)[:, :, 0:nb],
                   amask[0:nb, 0:nb].unsqueeze(1).to_broadcast([nb, 4, nb]), ALU.mult, [pak, "cf"], [atk])
                return [(atm, atk)]

            def emit_oi(blk, res):
                ob = pob(blk)
                po, pok = psb[ob], ("ps", ob)
                for hh, h in enumerate(hs):
                    if kind == "p":
                        atm, atk = res[hh // 2]
                        hl = hh % 2
                        MM(po[:, hh * 128:hh * 128 + 128], vtm[:, blk, hh, :], atm[:, (2 * hl) * 128:(2 * hl + 1) * 128],
                           hh == 0, False, [("vtm", blk), atk], [pok], skip_group_check=True)
                        MM(po[:, hh * 128:hh * 128 + 128], vtm[:, blk, hh, :],
                           atm[:, (2 * hl + 1) * 128:(2 * hl + 2) * 128],
                           False, False, [("vtm", blk), atk], [pok], skip_group_check=True)
                    else:
                        atm, atk = res[0]
                        MM(po[:, hh * 128:hh * 128 + nb], vtm[0:nb, blk, hh, :], atm[0:nb, hh * 128:hh * 128 + nb],
                           hh == 0, False, [("vtm", blk), atk], [pok], skip_group_check=True)

            nxtA = emit_A(0)
            emit_oi(0, nxtA)
            if kind == "p":
                steps = [(blk, c) for blk in range(nblk) for c in range(2)]

                def emit_KV(gs):
                    blk, c = steps[gs]
                    kvb = 6 + gs % 2
                    pk, pkk = psb[kvb], ("ps", kvb)
                    for hh, h in enumerate(hs):
                        MM(pk[:, hh * 128:(hh + 1) * 128], khtm[c * 64:(c + 1) * 64, blk, hh, :],
                           vtm[c * 64:(c + 1) * 64, blk, hh, :],
                           True, True, [("khtm", hh), ("vtm", blk)], [pkk], tile_position=(c * 64, 0))
                emit_KV(0)
                assert len(steps) % 2 == 0
                for gs, (blk, c) in enumerate(steps):
                    ob = pob(blk)
                    po, pok = psb[ob], ("ps", ob)
                    bc0 = blk * nb
                    ci = blk * cpb + 2 * c
                    if gs + 1 < len(steps):
                        emit_KV(gs + 1)
                    for hh, h in enumerate(hs):
                        MM(po[:, hh * 128 + c * 64:hh * 128 + (c + 1) * 64], Sbf[:, hh, :],
                           qx[:, hh, bc0 + c * 64:bc0 + (c + 1) * 64],
                           False, c == 1, [("Sbf", h), ("qx", hh)], [pok], skip_group_check=True)
                    kvb = 6 + gs % 2
                    pk, pkk = psb[kvb], ("ps", kvb)
                    for hh, h in enumerate(hs):
                        bufs = [(Sst[:, jl, h, :], Skh(h)), (S2[:, hh, :], ("S2", hh))]
                        (src, srck), (dst, dstk) = bufs[gs % 2], bufs[(gs + 1) % 2]
                        STT(dst, src, D2[:, h, ci:ci + 1], pk[:, hh * 128:(hh + 1) * 128],
                            ALU.mult, ALU.add, [srck, ("D2", h), pkk], [dstk])
                    if gs % 2 == 0:
                        ACT(Sbf[:, 0:4, :], S2[:, :, :], AF.Copy, [("S2", x) for x in range(4)],
                            [("Sbf", h) for h in hs])
                    else:
                        ACT(Sbf[:, 0:4, :], Sst[:, jl, 4 * hf:4 * hf + 4, :], AF.Copy,
                            [Skh(h) for h in hs], [("Sbf", h) for h in hs])
                    ada_tick()
                    if c == 0:
                        if blk > 0:
                            norm_sq(blk - 1)
                        if blk + 1 < nblk:
                            nxtA = emit_A(blk + 1)
                        if blk > 1:
                            norm_p3(blk - 2)
                    if c == 1:
                        if blk > 0:
                            norm_nm(blk - 1)
                            norm_p2(blk - 1)
                        if blk + 1 < nblk:
                            emit_oi(blk + 1, nxtA)
                if nblk > 1:
                    norm_p3(nblk - 2)
            else:
                po, pok = psb[pob(0)], ("ps", pob(0))
                for c in range(cpb):
                    i = c % NSTG
                    st_ = stgA[i]
                    sbs, sbsk = br.next()
                    ACT(sbs[:, 0:512], st_.rearrange("p h v -> p (h v)"), AF.Copy, stgk(i), [sbsk])
                    khm, khmk = br.next()
                    ACT(khm[0:nb, 0:512], khtm[0:nb, 0, :, :].rearrange("p h k -> p (h k)"), AF.Identity,
                        [("khtm", x) for x in range(4)] + ["cf"], [khmk], scale=cf[0:nb, C_RM + c:C_RM + c + 1])
                    kvb = 6 + c % 2
                    pk, pkk = psb[kvb], ("ps", kvb)
                    for hh, h in enumerate(hs):
                        MM(pk[:, hh * 128:(hh + 1) * 128], khm[0:nb, hh * 128:(hh + 1) * 128],
                           vtm[0:nb, 0, hh, :], True, True, [khmk, ("vtm", 0)], [pkk])
                    for hh, h in enumerate(hs):
                        MM(po[:, hh * 128 + c * L:hh * 128 + (c + 1) * L], sbs[:, hh * 128:(hh + 1) * 128],
                           qt[:, hh, c * L:(c + 1) * L],
                           False, c == cpb - 1, [sbsk, ("qt", hh)], [pok], skip_group_check=True)
                    for hh, h in enumerate(hs):
                        STT(st_[:, hh, :], st_[:, hh, :], dch[:, h, c:c + 1], pk[:, hh * 128:(hh + 1) * 128],
                            ALU.mult, ALU.add, stgk(i) + [("dch", h), pkk], stgk(i))
                    DMA("pool", hgS[jl, c, 4 * hf:4 * hf + 4].rearrange("h k v -> k h v"), st_,
                        stgk(i), [], ("stgst", i), store=True)
                    if c + NSTG < cpb:
                        ld(c + NSTG)
            norm_p1(nblk - 1)
            norm_p2(nblk - 1)
            norm_p3(nblk - 1)
        proj.banks = [0, 1, 2, 3, 4, 5, 6, 7]
        if next_h is not None:
            next_h()
        stage_out(g, l, ti, c0, n, kind)
        if last:
            for _ in range(8):
                woutS.load_next()

    def final_tile(g, ti, c0, n, kind):
        rstd, rk = stage_norm(DEPTH, ti, c0, n, kind)
        for c in range(8):
            STT(xT[:, c, c0:c0 + n], xT[:, c, c0:c0 + n], vecs[:, OFF_FG + c:OFF_FG + c + 1], rstd[:, 0:n],
                ALU.mult, ALU.mult, [("x", c, ti), "vecs", rk], [("x", c, ti)])
        oc0 = g * GTOK + c0 if kind == "p" else SEQ
        DMA("sp", yout[:, :, oc0:oc0 + n], xT[:, :, c0:c0 + n], xkeys(ti), [], ("st_y", ti), store=True)
        if g == 0 and kind == "p":
            DMA("sp", xT[:, :, c0:c0 + n], xin[:, :, GTOK + c0:GTOK + c0 + n], [],
                [("x", c, ti) for c in range(8)], ("ld_x", ti))

    if _DEBUG_STOP is None:
        for t_ in range(2):
            DMA("sp", xT[:, :, 512 * t_:512 * (t_ + 1)], xin[:, :, 512 * t_:512 * (t_ + 1)], [],
                [("x", c, t_) for c in range(8)], ("ld_x", t_))
        DMA("sp", xT[:, :, GTOK:XCOLS], xin[:, :, SEQ:SEQ + NSAMP], [], [("x", c, 2) for c in range(8)], "ld_xs")
    for g, tiles in enumerate(GROUPS if _DEBUG_STOP is None else []):
        seq = [(l, ti) + tuple(t) for l in range(DEPTH) for ti, t in enumerate(tiles)]

        def mk_next(idx):
            if idx + 1 >= len(seq):
                if g == 0:
                    c0N, nN, kindN = GROUPS[1][0]
                    return lambda: stage_h(0, 0, c0N, nN, kindN)
                return None
            lN, tiN, c0N, nN, kindN = seq[idx + 1]
            cross = lN != seq[idx][0]

            def f():
                if cross:
                    while ada_gen[0] is not None:
                        ada_tick()
                stage_h(lN, tiN, c0N, nN, kindN)
            return f
        for idx, (l, ti, c0, n, kind) in enumerate(seq):
            if ti == 0 and g == 0 and l + 1 < DEPTH:
                if l == 0:
                    def _chain():
                        yield from ada_steps(0)
                        yield from ada_steps(1)
                    ada_gen[0] = _chain()
                else:
                    ada_gen[0] = ada_steps(l + 1)
            last = ti == len(tiles) - 1
            fn = conv_tile if l % 2 == 0 else hgrn_tile
            fn(g, l, ti, c0, n, kind, last, pre_h=(idx > 0 or g > 0), next_h=mk_next(idx))
            if last:
                while ada_gen[0] is not None:
                    ada_tick()
                if g == 1 and l % 2 == 1:
                    jl = l // 2
                    DMA("sp", hgP[jl].rearrange("h k v -> k h v"), Sst[:, jl, :, :],
                        [("Sst", jl, x) for x in range(8)], [], ("st_hgP", jl), store=True)
            if l == DEPTH - 1:
                final_tile(g, ti, c0, n, kind)
    if _DEBUG_STOP is None:
      DMA("sp", convP, ustate[:], ["ustate%d_%d" % (a, b) for a in range(2) for b in range(8)], [], "st_cp", store=True)

    if _DEBUG_STOP is not None:
        g, l, ti, stage = _DEBUG_STOP
        c0, n, kind = GROUPS[g][ti]
        DMA("sp", xT[:, :, 0:GTOK], xin[:, :, g * GTOK:(g + 1) * GTOK], [],
            [("x", c, t) for c in range(8) for t in range(2)], "ld_x")
        if stage == "h":
            stage_h(l, ti, c0, n, kind)
        elif l % 2 == 0:
            conv_tile(g, l, ti, c0, n, kind, False)
        else:
            hgrn_tile(g, l, ti, c0, n, kind, False)
    S.emit(nc, es)
    es.close()
    return nc


_NC = [None]
_DEBUG_STOP = None


def _consts():
    cf = np.zeros((128, NCF), np.float32)
    t = np.arange(512)
    cf[:, C_SM32:C_SM32 + 512] = (t % 32 == 0).astype(np.float32)[None, :]
    cf[:, C_SM4:C_SM4 + 64] = (np.arange(64) % 4 == 0).astype(np.float32)[None, :]
    s = np.arange(128)[:, None]
    tt = np.arange(128)[None, :]
    cf[:, C_AM32:C_AM32 + 128] = ((s <= tt) & (s // 32 == tt // 32)).astype(np.float32)
    cf[:, C_AM2:C_AM2 + 128] = (((s // 32) % 2 == 0) & (tt // 32 == s // 32 + 1)).astype(np.float32)
    s4 = np.arange(64)[:, None]
    t4 = np.arange(64)[None, :]
    cf[:64, C_AM4:C_AM4 + 64] = ((s4 <= t4) & (s4 // 4 == t4 // 4)).astype(np.float32)
    cf[:64, C_RM:C_RM + 16] = (np.arange(64)[:, None] // 4 == np.arange(16)[None, :]).astype(np.float32)
    cf[:, C_ID:C_ID + 128] = np.eye(128, dtype=np.float32)
    return cf


def _fm(v):
    a = np.asarray(v, np.float32)
    a = a.reshape(a.shape[:-1] + (8, 128))
    return np.moveaxis(a, -1, 0)


def kernel(x_prompt, x_sample, state_conv, state_hgrn, c_prompt, c_sample, norm_g, w_ada, b_ada,
           conv_w_in, conv_w, conv_w_out, hgrn_w_in, hgrn_lower_bounds, hgrn_onorm_g, hgrn_w_out,
           final_norm_g):
    f = lambda a: np.ascontiguousarray(np.asarray(a, dtype=np.float32))
    x_prompt, x_sample, state_conv, state_hgrn = f(x_prompt), f(x_sample), f(state_conv), f(state_hgrn)
    c_prompt, c_sample = f(c_prompt), f(c_sample)
    cwi = f(conv_w_in).reshape(2, D, 4, 8, 128).transpose(0, 1, 3, 2, 4).reshape(2, D, 4 * D)
    hwi = f(hgrn_w_in).reshape(2, D, 4, 8, 128).transpose(0, 1, 3, 2, 4).reshape(2, D, 4 * D)
    w_in = np.ascontiguousarray(np.stack([cwi[0], hwi[0], cwi[1], hwi[1]]))
    cwo, hwo = f(conv_w_out), f(hgrn_w_out)
    w_out = np.ascontiguousarray(np.stack([cwo[0], hwo[0], cwo[1], hwo[1]]))
    w_ada_ = np.ascontiguousarray(f(w_ada).reshape(DEPTH, 8, 128, 24, 128).transpose(0, 3, 2, 1, 4)).reshape(
        DEPTH, 24, 128, 1024)
    vecs = np.zeros((128, NV), np.float32)
    vecs[:, OFF_NG:OFF_NG + 32] = _fm(norm_g).reshape(128, 32)
    vecs[:, OFF_FG:OFF_FG + 8] = _fm(final_norm_g).reshape(128, 8)
    vecs[:, OFF_BA:OFF_BA + 96] = np.moveaxis(f(b_ada).reshape(4, 24, 128), -1, 0).reshape(128, 96)
    vecs[:, OFF_CW:OFF_CW + 48] = _fm(conv_w).reshape(128, 48)
    vecs[:, OFF_LB:OFF_LB + 32] = _fm(hgrn_lower_bounds).reshape(128, 32)
    vecs[:, OFF_OG:OFF_OG + 16] = _fm(hgrn_onorm_g).reshape(128, 16)
    cf = _consts()
    in_maps = []
    for i in range(NCORES):
        xs = x_sample[NSEQ_S * i:NSEQ_S * (i + 1)].reshape(NSAMP, D)
        xa = np.concatenate([x_prompt[i], xs], axis=0)
        xin = np.ascontiguousarray(xa.reshape(SEQ + NSAMP, 8, 128).transpose(2, 1, 0))
        ca = np.concatenate([c_prompt[i:i + 1], c_sample[NSEQ_S * i:NSEQ_S * (i + 1)]], axis=0)
        cT = np.ascontiguousarray(ca.reshape(17, 8, 128).transpose(2, 1, 0)).reshape(128, 8 * 17)
        sc = state_conv[:, NSEQ_S * i:NSEQ_S * (i + 1)]
        scv = np.ascontiguousarray(sc.reshape(2, NSEQ_S, 2, 8, 128).transpose(4, 0, 3, 1, 2)).reshape(128, 512)
        shg = np.ascontiguousarray(state_hgrn[:, NSEQ_S * i:NSEQ_S * (i + 1)])
        in_maps.append({"xin": xin, "cT": cT, "vecs": vecs, "consts": cf, "scv": scv, "shg": shg,
                        "w_ada": w_ada_, "w_in": w_in, "w_out": w_out})
    if _NC[0] is None:
        _NC[0] = build_nc()
    res = run_bass_kernel_spmd(_NC[0], in_maps, core_ids=list(range(NCORES)))
    y_prompt = np.zeros((8, SEQ, D), np.float32)
    y_sample = np.zeros((128, TS_, D), np.float32)
    conv_p = np.zeros((2, 8, 2, D), np.float32)
    conv_s = np.zeros((2, 128, 2, D), np.float32)
    hg_p = np.zeros((2, 8, 8, 128, 128), np.float32)
    hg_s = np.zeros((2, 128, 8, 128, 128), np.float32)
    for i in range(NCORES):
        r = res.results[i]
        yo = np.asarray(r["yout"]).reshape(128, 8, SEQ + NSAMP).transpose(2, 1, 0).reshape(SEQ + NSAMP, D)
        y_prompt[i] = yo[:SEQ]
        y_sample[NSEQ_S * i:NSEQ_S * (i + 1)] = yo[SEQ:].reshape(NSEQ_S, TS_, D)
        cp = np.asarray(r["convP"]).reshape(128, 2, 8, 2)
        conv_p[:, i] = cp.transpose(1, 3, 2, 0).reshape(2, 2, D)
        cs = np.asarray(r["convS"]).reshape(128, 2, 8, NSEQ_S, 2)
        conv_s[:, NSEQ_S * i:NSEQ_S * (i + 1)] = cs.transpose(1, 3, 4, 2, 0).reshape(2, NSEQ_S, 2, D)
        hg_p[:, i] = np.asarray(r["hgP"]).reshape(2, 8, 128, 128)
        hg_s[:, NSEQ_S * i:NSEQ_S * (i + 1)] = np.asarray(r["hgS"]).reshape(2, NSEQ_S, 8, 128, 128)
    return (y_prompt, y_sample, conv_p, hg_p, conv_s, hg_s)
```
